# Optimizing a Trainium2 kernel written in Bass

```python
import math
import jax, jax.numpy as jnp
from jax import lax
import numpy as np

D_MODEL = 1024
BATCH = 4
SEQ = 8192
DEPTH = 2
DEC_BATCH = 4
DEC_SEQ = 4096
PAST_LEN = 128

GRID_W = 64
MIX_W = D_MODEL
HY_W = MIX_W // 2
ATT_W = MIX_W - HY_W
HEAD_DIM = 64
N_HEADS = ATT_W // HEAD_DIM
WIN_ROWS = 8
WIN_COLS = 16
SHORT_CONV = 3
POS_EMB_DIM = 33
POS_BANDS = (POS_EMB_DIM - 1) // 2
FILTER_HIDDEN = 64
FAST_DECAY_PCT = 0.3
SLOW_DECAY_PCT = 1.5
DECAY_TARGET = 1e-2
EPS = 1e-6
N_IN = 4 * HY_W + 4 * ATT_W

kernel_name = "hymba_hyena_natten_encoder"


def rmsnorm(x, g):
    xf = x.astype(jnp.float32)
    y = xf * lax.rsqrt(jnp.mean(xf * xf, axis=-1, keepdims=True) + EPS)
    return (y * g.astype(jnp.float32)).astype(x.dtype)


def short_conv(u, w, b):
    p = jnp.pad(u, ((0, 0), (1, 1), (0, 0)))
    return w[0] * p[:, :-2] + w[1] * p[:, 1:-1] + w[2] * p[:, 2:] + b


def hyena_filter(L, w1, b1, fr1, w2, b2, fr2, w3):
    f32 = jnp.float32
    t01 = jnp.linspace(0.0, 1.0, L, dtype=f32)[:, None]
    w = 2.0 * math.pi * jnp.arange(L, dtype=f32)[:, None] / L
    f = jnp.linspace(1e-4, POS_BANDS - 1, POS_BANDS, dtype=f32)[None, :]
    z = jnp.concatenate([t01, jnp.cos(f * w), -jnp.sin(f * w)], axis=-1)
    h = jnp.sin(fr1.astype(f32) * (z @ w1.astype(f32) + b1.astype(f32)))
    h = jnp.sin(fr2.astype(f32) * (h @ w2.astype(f32) + b2.astype(f32)))
    h = h @ w3.astype(f32)
    min_decay = math.log(DECAY_TARGET) / SLOW_DECAY_PCT
    max_decay = math.log(DECAY_TARGET) / FAST_DECAY_PCT
    deltas = jnp.linspace(min_decay, max_decay, 2 * HY_W, dtype=f32)[None, :]
    h = h * jnp.exp(-t01 * jnp.abs(deltas))
    h_fwd, h_bwd = h[:, :HY_W], h[:, HY_W:]
    return jnp.concatenate([h_fwd, jnp.zeros((1, HY_W), f32), h_bwd[:0:-1]], axis=0)


def long_conv(u, kfilt, bias):
    L = u.shape[1]
    uf = u.astype(jnp.float32)
    U = jnp.fft.rfft(uf, n=2 * L, axis=1)
    K = jnp.fft.rfft(kfilt, n=2 * L, axis=0)
    y = jnp.fft.irfft(U * K[None], n=2 * L, axis=1)[:, :L]
    return (y + uf * bias.astype(jnp.float32)).astype(u.dtype)


def neighbourhood_attention(q, k, v, rpb):
    B, L = q.shape[0], q.shape[1]
    rows = L // GRID_W
    kr = min(WIN_ROWS, rows)
    grid = lambda a: a.reshape(B, rows, GRID_W, N_HEADS, HEAD_DIM)
    qg, kg, vg = grid(q), grid(k), grid(v)
    cols = np.arange(GRID_W)
    col_start = np.clip(cols - WIN_COLS // 2, 0, GRID_W - WIN_COLS)
    col_idx = col_start[:, None] + np.arange(WIN_COLS)[None, :]
    col_off = col_idx - cols[:, None] + (WIN_COLS - 1)
    rpb_c = rpb.astype(jnp.float32)[:, :, col_off]
    scale = HEAD_DIM ** -0.5

    def one_row(r):
        r0 = jnp.clip(r - kr // 2, 0, rows - kr)
        q_r = lax.dynamic_index_in_dim(qg, r, axis=1, keepdims=False)
        k_band = lax.dynamic_slice_in_dim(kg, r0, kr, axis=1)
        v_band = lax.dynamic_slice_in_dim(vg, r0, kr, axis=1)
        k_win = k_band[:, :, col_idx]
        v_win = v_band[:, :, col_idx]
        row_off = r0 + jnp.arange(kr) - r + (WIN_ROWS - 1)
        bias = rpb_c[:, row_off].transpose(0, 2, 1, 3)
        s = jnp.einsum('bchd,bkcwhd->bhckw', q_r, k_win).astype(jnp.float32) * scale + bias
        p = jax.nn.softmax(s.reshape(B, N_HEADS, GRID_W, kr * WIN_COLS), axis=-1)
        p = p.reshape(B, N_HEADS, GRID_W, kr, WIN_COLS).astype(v.dtype)
        return jnp.einsum('bhckw,bkcwhd->bchd', p, v_win)

    out = lax.map(one_row, jnp.arange(rows))
    return out.transpose(1, 0, 2, 3, 4).reshape(B, L, N_HEADS * HEAD_DIM)


def encoder_layer(x, norm_g, w_in, conv_w, conv_b, f_w1, f_b1, f_fr1, f_w2, f_b2, f_fr2,
                  f_w3, hy_bias, qn_g, kn_g, rpb, on_hy, on_att, w_out):
    B, L, _ = x.shape
    h = rmsnorm(x, norm_g)
    z = h @ w_in
    hy_in, hy_gate, att_qkv, att_gate = jnp.split(
        z, [3 * HY_W, 4 * HY_W, 4 * HY_W + 3 * ATT_W], axis=-1)
    hy_in = short_conv(hy_in, conv_w, conv_b)
    x0, x1, vv = jnp.split(hy_in, 3, axis=-1)
    kfilt = hyena_filter(L, f_w1, f_b1, f_fr1, f_w2, f_b2, f_fr2, f_w3)
    y_hy = x0 * long_conv(x1 * vv, kfilt, hy_bias)
    y_hy = rmsnorm(y_hy, on_hy) * jax.nn.silu(hy_gate)
    q, k, v = jnp.split(att_qkv, 3, axis=-1)
    q = rmsnorm(q.reshape(B, L, N_HEADS, HEAD_DIM), qn_g)
    k = rmsnorm(k.reshape(B, L, N_HEADS, HEAD_DIM), kn_g)
    v = v.reshape(B, L, N_HEADS, HEAD_DIM)
    y_att = neighbourhood_attention(q, k, v, rpb)
    y_att = rmsnorm(y_att, on_att) * jax.nn.silu(att_gate)
    return x + jnp.concatenate([y_hy, y_att], axis=-1) @ w_out


def trunk(x, norm_g, w_in, conv_w, conv_b, f_w1, f_b1, f_fr1, f_w2, f_b2, f_fr2, f_w3,
          hy_bias, qn_g, kn_g, rpb, on_hy, on_att, w_out):
    for i in range(DEPTH):
        x = encoder_layer(x, norm_g[i], w_in[i], conv_w[i], conv_b[i], f_w1[i], f_b1[i],
                          f_fr1[i], f_w2[i], f_b2[i], f_fr2[i], f_w3[i], hy_bias[i],
                          qn_g[i], kn_g[i], rpb[i], on_hy[i], on_att[i], w_out[i])
    return x


def setup_inputs(seed: int = 0) -> dict:
    key = jax.random.key(seed)
    ks = jax.random.split(key, 24)
    f32 = jnp.float32
    nrm = lambda k, shape, s: jax.random.normal(k, shape, f32) * s
    gain = lambda k, shape: 1.0 + 0.01 * jax.random.normal(k, shape, f32)
    return {
        "x_prompt": jax.random.normal(ks[0], (BATCH, SEQ, D_MODEL), f32),
        "x_sample": jax.random.normal(ks[1], (DEC_BATCH, DEC_SEQ, D_MODEL), f32),
        "norm_g": gain(ks[2], (DEPTH, D_MODEL)),
        "w_in": nrm(ks[3], (DEPTH, D_MODEL, N_IN), D_MODEL ** -0.5),
        "conv_w": nrm(ks[4], (DEPTH, SHORT_CONV, 3 * HY_W), SHORT_CONV ** -0.5),
        "conv_b": nrm(ks[5], (DEPTH, 3 * HY_W), 0.01),
        "f_w1": nrm(ks[6], (DEPTH, POS_EMB_DIM, FILTER_HIDDEN), POS_EMB_DIM ** -0.5),
        "f_b1": nrm(ks[7], (DEPTH, FILTER_HIDDEN), 0.1),
        "f_fr1": gain(ks[8], (DEPTH, FILTER_HIDDEN)),
        "f_w2": nrm(ks[9], (DEPTH, FILTER_HIDDEN, FILTER_HIDDEN), FILTER_HIDDEN ** -0.5),
        "f_b2": nrm(ks[10], (DEPTH, FILTER_HIDDEN), 0.1),
        "f_fr2": gain(ks[11], (DEPTH, FILTER_HIDDEN)),
        "f_w3": nrm(ks[12], (DEPTH, FILTER_HIDDEN, 2 * HY_W), FILTER_HIDDEN ** -0.5),
        "hy_bias": nrm(ks[13], (DEPTH, HY_W), 0.5),
        "qn_g": gain(ks[14], (DEPTH, HEAD_DIM)),
        "kn_g": gain(ks[15], (DEPTH, HEAD_DIM)),
        "rpb": nrm(ks[16], (DEPTH, N_HEADS, 2 * WIN_ROWS - 1, 2 * WIN_COLS - 1), 0.02),
        "on_hy": gain(ks[17], (DEPTH, HY_W)),
        "on_att": gain(ks[18], (DEPTH, ATT_W)),
        "w_out": nrm(ks[19], (DEPTH, MIX_W, D_MODEL), MIX_W ** -0.5),
    }


def reference(x_prompt, x_sample, norm_g, w_in, conv_w, conv_b, f_w1, f_b1, f_fr1, f_w2,
              f_b2, f_fr2, f_w3, hy_bias, qn_g, kn_g, rpb, on_hy, on_att, w_out):
    y_prompt = trunk(x_prompt, norm_g, w_in, conv_w, conv_b, f_w1, f_b1, f_fr1, f_w2, f_b2,
                     f_fr2, f_w3, hy_bias, qn_g, kn_g, rpb, on_hy, on_att, w_out)
    y_sample = trunk(x_sample, norm_g, w_in, conv_w, conv_b, f_w1, f_b1, f_fr1, f_w2, f_b2,
                     f_fr2, f_w3, hy_bias, qn_g, kn_g, rpb, on_hy, on_att, w_out)
    return (y_prompt, y_sample)
```

```python
import math
import numpy as np
import ml_dtypes
import concourse.bass as bass
import concourse.mybir as mybir
from concourse.bass_utils import run_bass_kernel_spmd

F32 = mybir.dt.float32
BF16 = mybir.dt.bfloat16
ALU = mybir.AluOpType
ACTF = mybir.ActivationFunctionType
AX = mybir.AxisListType
NPBF = ml_dtypes.bfloat16

NDS = 56
SQ = "act"
SEMB = 30000

T = 8192
SEG = 4096
NRING = 16384
EPS = 1e-6
PI = math.pi
DEPTH = 2
MASKV = -80.0
SPECIAL = {0: [0, 1, 2, 3], 1: [-1, 0, 1, 2], 30: [-2, -1, 0, 1, 2], 31: [-3, -2, -1, 0, 1, 2],
           32: [-2, -1, 0, 1, 2, 3], 33: [-2, -1, 0, 1, 2], 62: [-2, -1, 0, 1], 63: [-3, -2, -1, 0]}
SPEC_LIST = sorted(SPECIAL)
INTERIOR = [-2, -1, 0, 1, 2]


class Buf:
    __slots__ = ("writers", "readers")

    def __init__(self):
        self.writers = {}
        self.readers = {}


class Prog:
    def __init__(self, nc):
        self.nc = nc
        self.names = ["pe", "act", "dve", "pool", "sp"]
        self.lists = {e: [] for e in self.names}
        self.count = {e: 0 for e in self.names}
        self.seen = {e: {} for e in self.names}
        self.dma_next = 0
        self.dma_tgt = [0] * NDS
        self.sems = {}
        self._stack = []
        self._marks = []
        self.uid = 0

    def mark(self):
        self._marks.append(len(self._stack))

    def release(self):
        m = self._marks.pop()
        while len(self._stack) > m:
            self._stack.pop().__exit__(None, None, None)

    def sb(self, name, shape, dt):
        self.uid += 1
        cm = self.nc.sbuf_tensor("%s_%d" % (name, self.uid), list(shape), dt)
        t = cm.__enter__()
        self._stack.append(cm)
        return t

    def ps(self, name, shape, dt=F32):
        self.uid += 1
        cm = self.nc.psum_tensor("%s_%d" % (name, self.uid), list(shape), dt)
        t = cm.__enter__()
        self._stack.append(cm)
        return t

    def _sem(self, key):
        if key not in self.sems:
            self.sems[key] = self.nc.semaphore("s_%s_%s" % (key[0], key[1])).__enter__()
        return self.sems[key]

    def _collect(self, eng, r, w, pw):
        need = {}
        for b in r:
            for k, v in b.writers.items():
                if need.get(k, 0) < v:
                    need[k] = v
        for b in w:
            for d in (b.writers, b.readers):
                for k, v in d.items():
                    if need.get(k, 0) < v:
                        need[k] = v
        for b in pw:
            for k, v in b.readers.items():
                if need.get(k, 0) < v:
                    need[k] = v
        out = []
        seen = self.seen[eng]
        for k, v in need.items():
            if k == ("e", "pe") and eng == "pe":
                continue
            if seen.get(k, 0) >= v:
                continue
            seen[k] = v
            out.append((k, v))
        return out

    def _update(self, key, n, r, w, pw):
        for b in r:
            b.readers[key] = n
        for b in w:
            b.writers = {key: n}
            b.readers = {}
        for b in pw:
            b.writers[key] = n

    def op(self, eng, fn, r=(), w=(), pw=()):
        waits = self._collect(eng, r, w, pw)
        self.count[eng] += 1
        n = self.count[eng]
        key = ("e", eng)
        self.lists[eng].append((waits, fn, key, n))
        self._update(key, n, r, w, pw)

    def dma(self, q, out_ap, in_ap, r=(), w=(), pw=()):
        s = self.dma_next
        self.dma_next = (s + 1) % NDS
        waits = self._collect(q, r, w, pw)
        key = ("d", s)
        prev = self.dma_tgt[s]
        if prev and self.seen[q].get(key, 0) < prev:
            self.seen[q][key] = prev
            waits.append((key, prev))
        self.dma_tgt[s] += 16
        n = self.dma_tgt[s]
        self.lists[q].append((waits, lambda e: e.dma_start(out=out_ap, in_=in_ap), key, n))
        self._update(key, n, r, w, pw)

    def barrier(self):
        allk = {}
        for e in self.names:
            if self.count[e]:
                allk[("e", e)] = self.count[e]
        for s in range(NDS):
            if self.dma_tgt[s]:
                allk[("d", s)] = self.dma_tgt[s]
        for e in self.names:
            waits = []
            for k, v in allk.items():
                if k == ("e", e):
                    continue
                if self.seen[e].get(k, 0) >= v:
                    continue
                self.seen[e][k] = v
                waits.append((k, v))
            if waits:
                self.lists[e].append((waits, None, None, 0))

    def _semval(self, key, v):
        if key[0] == "d":
            return self._sem(key), v
        idx = self._sigidx[key[1]][v]
        c = (idx - 1) // SEMB
        return self._sem((key[1], c)), (idx - 1) % SEMB + 1

    def emit(self):
        nc = self.nc
        waited = {e: set() for e in self.names}
        for e in self.names:
            for (waits, fn, key, n) in self.lists[e]:
                for (k, v) in waits:
                    if k[0] == "e":
                        waited[k[1]].add(v)
        self._sigidx = {e: {n: i + 1 for i, n in enumerate(sorted(waited[e]))} for e in self.names}
        for e in self.names:
            for (waits, fn, key, n) in self.lists[e]:
                for (k, v) in waits:
                    self._semval(k, v)
                if key is not None and (key[0] == "d" or n in self._sigidx[key[1]]):
                    self._semval(key, n)
        with nc.Block() as block:
            regs = {"pe": block.tensor, "act": block.scalar, "dve": block.vector,
                    "pool": block.gpsimd, "sp": block.sync}
            for ename in self.names:
                def mk(lst):
                    def f(e):
                        for (waits, fn, key, n) in lst:
                            for (k, v) in waits:
                                s, val = self._semval(k, v)
                                e.wait_ge(s, val)
                            if fn is None:
                                continue
                            ins = fn(e)
                            if key[0] == "d":
                                s, val = self._semval(key, n)
                                ins.then_inc(s, 16)
                            elif n in self._sigidx[key[1]]:
                                s, val = self._semval(key, n)
                                ins.then_inc(s, 1)
                    return f
                regs[ename](mk(self.lists[ename]))


def _consts(is_prompt):
    c = {}
    L = 8192 if is_prompt else 4096
    n1map = np.arange(64) if is_prompt else np.where(np.arange(64) < 32, np.arange(64), np.arange(64) + 32)
    k1 = np.arange(65)
    ang = 2 * np.pi * np.outer(n1map, k1) / 128.0
    c["F1"] = np.concatenate([np.cos(ang), -np.sin(ang)], 1).astype(NPBF)
    ang = 2 * np.pi * np.outer(np.arange(128), k1) / 128.0
    c["F1f"] = np.concatenate([np.cos(ang), -np.sin(ang)], 1).astype(NPBF)
    n2 = np.arange(128)
    k2 = np.arange(128)
    kk = k1[None, :, None] + 128 * k2[None, None, :]
    ang = 2 * np.pi * (n2[:, None, None] * kk % NRING) / NRING
    c["Gr"] = np.cos(ang).astype(NPBF)
    c["Gi"] = (-np.sin(ang)).astype(NPBF)
    c["Gin"] = np.sin(ang).astype(NPBF)
    ang = 2 * np.pi * np.outer(k2, n2) / 128.0
    c["CS"] = np.concatenate([np.cos(ang), np.sin(ang)], 1).astype(NPBF)
    c["SC"] = np.concatenate([-np.sin(ang), np.cos(ang)], 1).astype(NPBF)
    wgt = np.where((k1 == 0) | (k1 == 64), 1.0, 2.0) / NRING
    n = 128 * n1map[None, None, :] + n2[None, :, None]
    ang = 2 * np.pi * ((n * k1[:, None, None]) % NRING) / NRING
    c["Hr"] = (wgt[:, None, None] * np.cos(ang)).astype(NPBF)
    c["Hi"] = (-wgt[:, None, None] * np.sin(ang)).astype(NPBF)
    pos = np.arange(NRING)
    valid = (pos < L) | (pos > NRING - L)
    tau = np.where(pos < L, pos, NRING - pos)
    tau = np.where(valid, tau, 0)
    t01 = np.linspace(0.0, 1.0, L, dtype=np.float32)[tau]
    w = (np.float32(2.0 * math.pi) * tau.astype(np.float32) / np.float32(L)).astype(np.float32)
    f = np.linspace(1e-4, 15, 16, dtype=np.float32)
    fw = (f[None, :] * w[:, None]).astype(np.float32)
    z = np.concatenate([t01[:, None], np.cos(fw), -np.sin(fw)], 1).astype(np.float32)
    c["zT"] = np.ascontiguousarray(z.T)
    t01m = np.where(valid, t01, np.float32(1e4)).astype(np.float32)
    c["T01"] = np.ascontiguousarray(np.broadcast_to(t01m[None, :], (128, NRING)))
    min_decay = math.log(1e-2) / 1.5
    max_decay = math.log(1e-2) / 0.3
    deltas = np.abs(np.linspace(min_decay, max_decay, 1024, dtype=np.float32))
    c["negd"] = np.ascontiguousarray((-deltas).reshape(8, 128).T.astype(np.float32))
    rows_seq = 128 if is_prompt else 64
    cols = np.arange(64)
    cstart = np.clip(cols - 8, 0, 48)
    colok = (cols[:, None] >= cstart[None, :]) & (cols[:, None] < cstart[None, :] + 16)

    def mask_tile(p, d):
        m = np.full((2, 64, 2, 64), MASKV * 8.0, np.float32)
        for a in range(2):
            r = 2 * p + a
            sbase = (r // rows_seq) * rows_seq
            rl = r - sbase
            r0 = min(max(rl - 4, 0), rows_seq - 8) + sbase
            for kr in range(2):
                krow = 2 * (p + d) + kr
                if r0 <= krow < r0 + 8:
                    m[kr, :, a, :] = np.where(colok, 0.0, MASKV * 8.0)
        return m.reshape(128, 128)
    c["mask_int"] = np.stack([mask_tile(10, d) for d in INTERIOR]).astype(NPBF)
    sp = []
    for p in SPEC_LIST:
        for d in SPECIAL[p]:
            sp.append(mask_tile(p, d))
    c["mask_sp"] = np.stack(sp).astype(NPBF)
    c["ident"] = np.eye(128).astype(NPBF)
    c["mflag"] = np.full((128, 1), 1.0 if is_prompt else 0.0, np.float32)
    return c


def _rpb_gather(rpb_l):
    out = np.zeros((7, 2, 64, 8, 2, 64), np.float32)
    kc = np.arange(64)[:, None]
    qc = np.arange(64)[None, :]
    co = kc - qc + 15
    cok = (co >= 0) & (co <= 30)
    coc = np.clip(co, 0, 30)
    for di, d in enumerate(range(-3, 4)):
        for kr in range(2):
            for a in range(2):
                ro = 2 * d + kr - a + 7
                if 0 <= ro <= 14:
                    for h in range(8):
                        out[di, kr, :, h, a, :] = np.where(cok, rpb_l[h, ro][coc], 0.0)
    return out.reshape(7, 128, 8, 128)


def build(debug=False):
    nc = bass.Bass("TRN2", target_bir_lowering=False)
    P = Prog(nc)
    D = {}

    def din(name, shape, dt=F32):
        D[name] = nc.dram_tensor(name, list(shape), dt, kind="ExternalInput").ap()
        return D[name]

    def dscr(name, shape, dt):
        kind = "ExternalOutput" if debug else "Internal"
        D[name] = nc.dram_tensor(name, list(shape), dt, kind=kind).ap()
        return D[name]

    din("x", [T, 1024])
    din("norm_g", [DEPTH, 128, 1024])
    din("w_in", [DEPTH, 1024, 4096])
    din("conv_w", [DEPTH, 128, 12, 3])
    din("conv_b", [DEPTH, 128, 12])
    din("f_w1", [DEPTH, 2, 33, 128])
    din("f_b1", [DEPTH, 128, 1])
    din("f_fr1", [DEPTH, 128, 1])
    din("f_w2", [DEPTH, 128, 128])
    din("f_b2", [DEPTH, 128, 1])
    din("f_fr2", [DEPTH, 128, 1])
    din("f_w3", [DEPTH, 128, 1024])
    din("hy_bias", [DEPTH, 128, 4])
    din("on_hy", [DEPTH, 128, 4])
    din("qn_g", [DEPTH, 128, 64])
    din("kn_g", [DEPTH, 128, 64])
    din("on_att", [DEPTH, 128, 512])
    din("rpbT", [DEPTH, 7, 128, 8, 128])
    din("w_out", [DEPTH, 1024, 1024])
    for nm, shp, dt in (("F1", [64, 130], BF16), ("F1f", [128, 130], BF16), ("Gr", [128, 65, 128], BF16),
                        ("Gi", [128, 65, 128], BF16), ("Gin", [128, 65, 128], BF16), ("CS", [128, 256], BF16),
                        ("SC", [128, 256], BF16), ("Hr", [65, 128, 64], BF16), ("Hi", [65, 128, 64], BF16),
                        ("zT", [33, NRING], F32), ("T01", [128, NRING], F32), ("negd", [128, 8], F32),
                        ("mask_int", [5, 128, 128], BF16), ("mask_sp", [38, 128, 128], BF16),
                        ("ident", [128, 128], BF16), ("mflag", [128, 1], F32)):
        din(nm, shp, dt)
    y_out = nc.dram_tensor("y", [T, 1024], F32, kind="ExternalOutput").ap()
    x1 = dscr("x1", [T, 1024], F32)
    uD = dscr("uD", [512, T], BF16)
    x0D = dscr("x0D", [512, T], BF16)
    gD = dscr("gD", [512, T], BF16)
    yD = dscr("yD", [512, T], F32)
    kfD = dscr("kfD", [512, NRING], BF16)
    KD = dscr("KD", [16, 128, 9 * 512], F32)
    QTD = dscr("QTD", [4, 128, T], BF16)
    KTD = dscr("KTD", [4, 128, T], BF16)
    VD = dscr("VD", [64, 128, 520], BF16)
    GD = dscr("GD", [64, 128, 512], BF16)
    ATD = dscr("ATD", [4, 128, T], BF16)

    bD = {k: Buf() for k in ("x1", "uD", "x0D", "gD", "yD", "kfD", "KD", "QTD", "KTD", "VD", "GD", "ATD", "y")}

    ident = P.sb("ident", [128, 128], BF16)
    b_ident = Buf()
    P.dma("sp", ident[:], D["ident"], w=[b_ident])
    rs_att = P.sb("rs_att", [128, 64], F32)
    b_rsatt = Buf()
    mflag = P.sb("mflag", [128, 1], F32)
    b_mflag = Buf()
    P.dma("sp", mflag[:], D["mflag"], w=[b_mflag])
    epsc = P.sb("epsc", [128, 1], F32)
    b_eps = Buf()
    P.op("pool", lambda e: e.memset(epsc[:], EPS), w=[b_eps])
    negpi = P.sb("negpi", [128, 1], F32)

    def run_pipelined(gens, width):
        gens = list(gens)
        active = []
        nxt = 0
        while nxt < len(gens) or active:
            if nxt < len(gens) and len(active) < width:
                active.append(gens[nxt])
                nxt += 1
            for g in list(active):
                try:
                    next(g)
                except StopIteration:
                    active.remove(g)

    def act(out, in_, func, r, w=(), pw=(), **kw):
        P.op("act", lambda e: e.activation(out=out, in_=in_, func=func, **kw), r=r, w=w, pw=pw)

    def rstd_from_ss(ss_ap, tmp_ap, out_ap, inv_n, bss, r=()):
        np_ = ss_ap.shape[0]
        P.op("act", lambda e: e.activation(out=tmp_ap, in_=ss_ap, func=ACTF.Sqrt, scale=inv_n, bias=epsc[0:np_, :]),
             r=[bss, b_eps] + list(r), w=[bss])
        P.op("dve", lambda e: e.reciprocal(out=out_ap, in_=tmp_ap), r=[bss], w=[bss])

    def stage_filter(l):
        P.mark()
        w1a = P.sb("w1a", [33, 128], F32)
        w1b = P.sb("w1b", [33, 128], F32)
        w2 = P.sb("w2", [128, 128], F32)
        w3f = P.sb("w3f", [128, 1024], F32)
        w3 = P.sb("w3", [128, 1024], BF16)
        sc = P.sb("fsc", [128, 12], F32)
        negd = P.sb("negd", [128, 8], F32)
        bw = Buf()
        P.dma("sp", w1a[:], D["f_w1"][l, 0], pw=[bw])
        P.dma("sp", w1b[:], D["f_w1"][l, 1], pw=[bw])
        P.dma("sp", w2[:], D["f_w2"][l], pw=[bw])
        P.dma("sp", w3f[:], D["f_w3"][l], pw=[bw])
        P.dma("sp", sc[:, 0:1], D["f_fr1"][l], pw=[bw])
        P.dma("sp", sc[:, 1:2], D["f_b1"][l], pw=[bw])
        P.dma("sp", sc[:, 3:4], D["f_fr2"][l], pw=[bw])
        P.dma("sp", sc[:, 4:5], D["f_b2"][l], pw=[bw])
        P.dma("sp", negd[:], D["negd"], pw=[bw])
        bw2 = Buf()
        P.op("pool", lambda e: e.tensor_copy(out=w3[:], in_=w3f[:]), r=[bw], pw=[bw2])
        P.op("pool", lambda e: e.tensor_tensor(out=sc[:, 2:3], in0=sc[:, 0:1], in1=sc[:, 1:2], op=ALU.mult), r=[bw], pw=[bw2])
        P.op("pool", lambda e: e.tensor_tensor(out=sc[:, 5:6], in0=sc[:, 3:4], in1=sc[:, 4:5], op=ALU.mult), r=[bw], pw=[bw2])
        bw3 = Buf()
        for (dst, src) in ((6, 0), (7, 2), (8, 3), (9, 5)):
            P.op("pool", lambda e, dst=dst, src=src: e.tensor_scalar(out=sc[:, dst:dst + 1], in0=sc[:, src:src + 1], scalar1=1.0 / 3.0, scalar2=None,
                                                                     op0=ALU.mult), r=[bw, bw2], pw=[bw3])
        bw2 = bw3 if False else bw2
        W = 4
        zt = [P.sb("zt", [33, 2, 512], F32) for _ in range(W)]
        bz = [Buf() for _ in range(W)]
        tt = [P.sb("t01t", [128, 2, 512], F32) for _ in range(W)]
        btt = [Buf() for _ in range(W)]
        ps1 = [P.ps("fps1", [128, 512]) for _ in range(W)]
        bps1 = [Buf() for _ in range(W)]
        ps3 = [P.ps("fps3", [128, 512]) for _ in range(W)]
        bps3 = [Buf() for _ in range(W)]
        tA = [P.sb("tA", [128, 512], F32) for _ in range(W)]; btA = [Buf() for _ in range(W)]
        tB = [P.sb("tB", [128, 512], F32) for _ in range(W)]; btB = [Buf() for _ in range(W)]
        tC = [P.sb("tC", [128, 512], F32) for _ in range(W)]; btC = [Buf() for _ in range(W)]
        h1 = [P.sb("h1", [128, 512], F32) for _ in range(W)]; bh1 = [Buf() for _ in range(W)]
        h2 = [P.sb("h2", [128, 512], BF16) for _ in range(W)]; bh2 = [Buf() for _ in range(W)]
        dec = [P.sb("dec", [128, 512], F32) for _ in range(W)]
        bdec = [Buf() for _ in range(W)]
        kf = [P.sb("kf", [128, 512], BF16) for _ in range(W)]
        bkf = [Buf() for _ in range(W)]

        def sin_layer(s, frc, fbc, out_ap, bout):
            act(tA[s][:], ps1[s][:], ACTF.Sin, r=[bps1[s], bw, bw2, bw3], w=[btA[s]], scale=sc[:, frc:frc + 1], bias=sc[:, fbc:fbc + 1])
            yield
            P.op("dve", lambda e: e.tensor_tensor(out=tB[s][:], in0=tA[s][:], in1=tA[s][:], op=ALU.mult), r=[btA[s]], w=[btB[s]])
            yield
            P.op("dve", lambda e: e.tensor_scalar(out=tB[s][:], in0=tB[s][:], scalar1=-4.0, scalar2=3.0,
                                                  op0=ALU.mult, op1=ALU.add), r=[btB[s]], w=[btB[s]])
            yield
            P.op("pool", lambda e: e.tensor_tensor(out=out_ap, in0=tB[s][:], in1=tA[s][:], op=ALU.mult), r=[btA[s], btB[s]], w=[bout])
            yield

        def block_gen(bp):
            s = bp % W
            for i, b in enumerate((bp, bp + 16)):
                P.dma("sp", zt[s][:, i, :], D["zT"][:, b * 512:(b + 1) * 512], **({"w": [bz[s]]} if i == 0 else {"pw": [bz[s]]}))
                P.dma("sp", tt[s][:, i, :], D["T01"][:, b * 512:(b + 1) * 512], **({"w": [btt[s]]} if i == 0 else {"pw": [btt[s]]}))
            P.op("pe", lambda e: e.matmul(ps1[s][:], lhsT=w1a[:], rhs=zt[s][:, 0, :], start=True, stop=False), r=[bw, bz[s]], w=[bps1[s]])
            P.op("pe", lambda e: e.matmul(ps1[s][:], lhsT=w1b[:], rhs=zt[s][:, 1, :], start=False, stop=True), r=[bw, bz[s]], pw=[bps1[s]])
            yield
            yield from sin_layer(s, 6, 7, h1[s][:], bh1[s])
            P.op("pe", lambda e: e.matmul(ps1[s][:], lhsT=w2[:], rhs=h1[s][:], start=True, stop=True), r=[bw, bh1[s]], w=[bps1[s]])
            yield
            yield from sin_layer(s, 8, 9, h2[s][:], bh2[s])
            for i, b in enumerate((bp, bp + 16)):
                for cc in range(4):
                    q = cc
                    col0 = i * 512 + cc * 128
                    ndc = i * 4 + cc
                    P.op("pe", lambda e, q=q, col0=col0: e.matmul(ps3[q][:], lhsT=w3[:, col0:col0 + 128], rhs=h2[s][:], start=True, stop=True),
                         r=[bw2, bh2[s]], w=[bps3[q]])
                    act(dec[q][:], tt[s][:, i, :], ACTF.Exp, r=[btt[s], bw], w=[bdec[q]], scale=negd[:, ndc:ndc + 1])
                    yield
                    P.op("dve", lambda e, q=q: e.tensor_tensor(out=kf[q][:], in0=ps3[q][:], in1=dec[q][:], op=ALU.mult),
                         r=[bps3[q], bdec[q]], w=[bkf[q]])
                    P.dma(SQ, kfD[cc * 128:(cc + 1) * 128, b * 512:(b + 1) * 512], kf[q][:], r=[bkf[q]], pw=[bD["kfD"]])
                    yield

        run_pipelined([block_gen(bp) for bp in range(16)], W)
        P.barrier()
        P.release()

    def fft_consts():
        t = {}
        bc = Buf()
        for nm, shp in (("F1", [64, 130]), ("F1f", [128, 130]), ("Gr", [128, 65, 128]), ("Gi", [128, 65, 128]),
                        ("Gin", [128, 65, 128]), ("CS", [128, 256]), ("SC", [128, 256]), ("Hr", [65, 128, 64]),
                        ("Hi", [65, 128, 64])):
            t[nm] = P.sb("c" + nm, shp, BF16)
            P.dma("sp", t[nm][:], D[nm], pw=[bc])
        return t, bc

    def fft_bufs(kpart):
        fb = {}
        fb["Vc"] = [P.sb("Vc", [kpart, 32, 128], BF16) for _ in range(2)]
        fb["bV"] = [Buf(), Buf()]
        fb["psA"] = [P.ps("psA", [128, 3, 130]) for _ in range(2)]
        fb["bpsA"] = [Buf(), Buf()]
        fb["psU"] = [P.ps("psU", [128, 8, 2, 32]) for _ in range(2)]
        fb["bpsU"] = [Buf(), Buf()]
        fb["A_sb"] = P.sb("A_sb", [128, 32, 130], BF16)
        fb["bA"] = Buf()
        return fb

    def fft_forward(fb, bc, src, bsrc, g, F1t):
        vq = g % 2
        Vc, bV = fb["Vc"][vq], fb["bV"][vq]
        psA, bpsA, A_sb, bA = fb["psA"], fb["bpsA"], fb["A_sb"], fb["bA"]
        P.dma("sp", Vc[:], src[g * 32:(g + 1) * 32, :].rearrange("c (j n) -> j c n", n=128), r=[bsrc], w=[bV])
        for c0 in range(0, 32, 3):
            nn = min(3, 32 - c0)
            q = (c0 // 3) % 2
            for i in range(nn):
                P.op("pe", lambda e, q=q, i=i, c0=c0: e.matmul(psA[q][:, i, :], lhsT=Vc[:, c0 + i, :], rhs=F1t[:], start=True, stop=True),
                     r=[bV, bc], **({"w": [bpsA[q]]} if i == 0 else {"pw": [bpsA[q]]}))
            act(A_sb[:, c0:c0 + nn, :], psA[q][:, 0:nn, :], ACTF.Copy, r=[bpsA[q]], pw=[bA])

    def fft_stage2(C, bc, fb, evac):
        psU, bpsU, A_sb, bA = fb["psU"], fb["bpsU"], fb["A_sb"], fb["bA"]
        for bk in range(9):
            q = bk % 2
            nk = min(8, 65 - bk * 8)
            first = True
            for kk in range(nk):
                k1 = bk * 8 + kk
                Ar = A_sb[:, :, k1]
                Ai = A_sb[:, :, 65 + k1]
                for (o, lt, rh, st) in ((0, C["Gr"], Ar, True), (0, C["Gin"], Ai, False), (1, C["Gi"], Ar, True), (1, C["Gr"], Ai, False)):
                    P.op("pe", lambda e, q=q, kk=kk, o=o, lt=lt, rh=rh, st=st, k1=k1: e.matmul(
                        psU[q][:, kk, o, :], lhsT=lt[:, k1, :], rhs=rh, start=st, stop=not st),
                        r=[bA, bc], **({"w": [bpsU[q]]} if first else {"pw": [bpsU[q]]}))
                    first = False
            evac(bk, nk, psU[q], bpsU[q])

    def stage_filter_fft(l, C, bc):
        P.mark()
        fb = fft_bufs(128)
        Ko = [P.sb("Ko", [128, 512], F32) for _ in range(2)]
        bKo = [Buf(), Buf()]
        for g in range(16):
            fft_forward(fb, bc, kfD, bD["kfD"], g, C["F1f"])

            def evac(bk, nk, ps, bps, g=g):
                q = bk % 2
                act(Ko[q][:, 0:nk * 64], ps[:, 0:nk, :, :].rearrange("p a b c -> p (a b c)"), ACTF.Copy, r=[bps], w=[bKo[q]])
                P.dma(SQ, KD[g, :, bk * 512:bk * 512 + nk * 64], Ko[q][:, 0:nk * 64], r=[bKo[q]], pw=[bD["KD"]])
            fft_stage2(C, bc, fb, evac)
        P.barrier()
        P.release()

    def stage_conv(l, C, bc):
        P.mark()
        fb = fft_bufs(64)
        Ks = [P.sb("Ks", [128, 9 * 512], F32) for _ in range(2)]; bKs = [Buf(), Buf()]
        Us = [P.sb("Us", [128, 512], F32) for _ in range(2)]; bUs = [Buf(), Buf()]
        t1 = [P.sb("t1", [128, 256], F32) for _ in range(2)]; bt1 = [Buf(), Buf()]
        t2 = [P.sb("t2", [128, 256], F32) for _ in range(2)]; bt2 = [Buf(), Buf()]
        Yrr = [P.sb("Yr", [128, 32, 65], BF16) for _ in range(2)]
        Yir = [P.sb("Yi", [128, 32, 65], BF16) for _ in range(2)]
        bYr = [Buf(), Buf()]
        Dsb = P.sb("Dsb", [65, 32, 2, 128], BF16); bDs = Buf()
        Yo = P.sb("Yo", [64, 32, 128], F32); bYo = Buf()
        psD = [P.ps("psD", [65, 2, 256]) for _ in range(2)]
        bpsD = [Buf(), Buf()]
        psY = [P.ps("psY", [64, 16, 32]) for _ in range(2)]
        bpsY = [Buf(), Buf()]
        def fwd1(g):
            kq = g % 2
            P.dma("sp", Ks[kq][:, 0:4160], KD[g, :, 0:4160], r=[bD["KD"]], w=[bKs[kq]])
            fft_forward(fb, bc, uD, bD["uD"], g, C["F1"])

        def fwd2(g):
            kq = g % 2
            Yr, Yi, bY = Yrr[g % 2], Yir[g % 2], bYr[g % 2]

            def evac(bk, nk, ps, bps, kq=kq):
                q = bk % 2
                eng = "pool" if bk in (1, 3, 5, 7) else "dve"
                n = nk * 64
                if eng == "pool":
                    act(Us[q][:, 0:n], ps[:, 0:nk, :, :].rearrange("p a b c -> p (a b c)"), ACTF.Copy, r=[bps], w=[bUs[q]])
                    U4 = Us[q][:, 0:n].rearrange("p (a b c) -> p a b c", b=2, c=32)
                    bu_ = bUs[q]
                else:
                    U4 = ps[:, 0:nk, :, :]
                    bu_ = bps
                K4 = Ks[kq][:, bk * 512:bk * 512 + n].rearrange("p (a b c) -> p a b c", b=2, c=32)
                T1 = t1[q][:, 0:nk * 32].rearrange("p (a c) -> p a c", c=32)
                T2 = t2[q][:, 0:nk * 32].rearrange("p (a c) -> p a c", c=32)
                yr = Yr[:, :, bk * 8:bk * 8 + nk].rearrange("p c a -> p a c")
                yi = Yi[:, :, bk * 8:bk * 8 + nk].rearrange("p c a -> p a c")
                bk_ = bKs[kq]
                P.op(eng, lambda e: e.tensor_tensor(out=T1, in0=U4[:, :, 0, :], in1=K4[:, :, 0, :], op=ALU.mult), r=[bu_, bk_], w=[bt1[q]])
                P.op(eng, lambda e: e.tensor_tensor(out=T2, in0=U4[:, :, 1, :], in1=K4[:, :, 1, :], op=ALU.mult), r=[bu_, bk_], w=[bt2[q]])
                P.op(eng, lambda e: e.tensor_tensor(out=yr, in0=T1, in1=T2, op=ALU.subtract), r=[bt1[q], bt2[q]], pw=[bY])
                P.op(eng, lambda e: e.tensor_tensor(out=T1, in0=U4[:, :, 0, :], in1=K4[:, :, 1, :], op=ALU.mult), r=[bu_, bk_], w=[bt1[q]])
                P.op(eng, lambda e: e.tensor_tensor(out=T2, in0=U4[:, :, 1, :], in1=K4[:, :, 0, :], op=ALU.mult), r=[bu_, bk_], w=[bt2[q]])
                P.op(eng, lambda e: e.tensor_tensor(out=yi, in0=T1, in1=T2, op=ALU.add), r=[bt1[q], bt2[q]], pw=[bY])
            fft_stage2(C, bc, fb, evac)

        def invA(g):
            Yr, Yi, bY = Yrr[g % 2], Yir[g % 2], bYr[g % 2]
            for c0 in range(0, 32, 2):
                q = (c0 // 2) % 2
                for i in range(2):
                    c = c0 + i
                    P.op("pe", lambda e, q=q, i=i, c=c: e.matmul(psD[q][:, i, :], lhsT=Yr[:, c, :], rhs=C["CS"][:], start=True, stop=False),
                         r=[bY, bc], **({"w": [bpsD[q]]} if i == 0 else {"pw": [bpsD[q]]}))
                    P.op("pe", lambda e, q=q, i=i, c=c: e.matmul(psD[q][:, i, :], lhsT=Yi[:, c, :], rhs=C["SC"][:], start=False, stop=True),
                         r=[bY, bc], pw=[bpsD[q]])
                act(Dsb[:, c0:c0 + 2, :, :].rearrange("p c b n -> p c (b n)"), psD[q][:], ACTF.Copy, r=[bpsD[q]], pw=[bDs])

        def final(g):
            for nb in range(8):
                q = nb % 2
                for i in range(16):
                    n2 = nb * 16 + i
                    P.op("pe", lambda e, q=q, i=i, n2=n2: e.matmul(psY[q][:, i, :], lhsT=C["Hr"][:, n2, :], rhs=Dsb[:, :, 0, n2], start=True, stop=False),
                         r=[bDs, bc], **({"w": [bpsY[q]]} if i == 0 else {"pw": [bpsY[q]]}))
                    P.op("pe", lambda e, q=q, i=i, n2=n2: e.matmul(psY[q][:, i, :], lhsT=C["Hi"][:, n2, :], rhs=Dsb[:, :, 1, n2], start=False, stop=True),
                         r=[bDs, bc], pw=[bpsY[q]])
                P.op("dve", lambda e, q=q, nb=nb: e.tensor_copy(out=Yo[:, :, nb * 16:(nb + 1) * 16].rearrange("p c n -> p n c"), in_=psY[q][:]),
                     r=[bpsY[q]], **({"w": [bYo]} if nb == 0 else {"pw": [bYo]}))
            P.dma(SQ, yD[g * 32:(g + 1) * 32, :].rearrange("c (j n) -> j c n", n=128), Yo[:], r=[bYo], pw=[bD["yD"]])

        fwd1(0)
        fwd2(0)
        for g in range(16):
            if g + 1 < 16:
                fwd1(g + 1)
            invA(g)
            if g + 1 < 16:
                fwd2(g + 1)
            final(g)
        P.barrier()
        P.release()

    def stage_proj(l, xsrc, bxsrc):
        P.mark()
        gt = P.sb("gt", [128, 1024], F32); bg = Buf()
        P.dma("sp", gt[:], D["norm_g"][l], w=[bg])
        cw = P.sb("cw", [128, 12, 3], F32)
        cb = P.sb("cb", [128, 12], F32)
        qg = P.sb("qg", [128, 64], F32)
        kg = P.sb("kg", [128, 64], F32)
        bsm = Buf()
        P.dma("sp", cw[:], D["conv_w"][l], pw=[bsm])
        P.dma("sp", cb[:], D["conv_b"][l], pw=[bsm])
        P.dma("sp", qg[:], D["qn_g"][l], pw=[bsm])
        P.dma("sp", kg[:], D["kn_g"][l], pw=[bsm])
        wA = P.sb("wA", [128, 8, 2048], BF16); bwA = Buf()
        wst = [P.sb("wst", [128, 8, 128], F32) for _ in range(2)]; bwst = [Buf(), Buf()]
        wv = D["w_in"][l].rearrange("(k p) c -> p k c", p=128)
        for j in range(16):
            q = j % 2
            P.dma("sp", wst[q][:], wv[:, :, 2048 + j * 128:2048 + (j + 1) * 128], w=[bwst[q]])
            P.op("act", lambda e, q=q, j=j: e.activation(out=wA[:, :, j * 128:(j + 1) * 128], in_=wst[q][:], func=ACTF.Copy), r=[bwst[q]], pw=[bwA])
        hT = P.sb("hT", [128, 8, SEG + 2], BF16); bhT = Buf()
        xt = [P.sb("xt", [128, 1024], F32) for _ in range(2)]; bx = [Buf(), Buf()]
        ss = [P.sb("ss", [128, 4], F32) for _ in range(2)]; bss = [Buf(), Buf()]
        hb = [P.sb("hb", [128, 1024], BF16) for _ in range(2)]; bh = [Buf(), Buf()]
        ptr = P.ps("ptr", [128, 8, 128], BF16); bptr = Buf()
        edge = P.sb("edge", [128, 8, 128], BF16); bedge = Buf()
        state = {"i": 0}

        def norm_tile(row0, dst_ap, bdst, dst_w):
            q = state["i"] % 2
            state["i"] += 1
            P.dma("sp", xt[q][:], xsrc[row0:row0 + 128, :], r=[bxsrc], w=[bx[q]])
            P.op("act", lambda e: e.activation(out=hb[q][:], in_=xt[q][:], func=ACTF.Square, accum_out=ss[q][:, 0:1]), r=[bx[q]], w=[bh[q], bss[q]])
            yield
            P.op("act", lambda e: e.activation(out=ss[q][:, 1:2], in_=ss[q][:, 0:1], func=ACTF.Sqrt, scale=1.0 / 1024, bias=epsc[:]),
                 r=[bss[q], b_eps], w=[bss[q]])
            yield
            P.op("dve", lambda e: e.reciprocal(out=ss[q][:, 2:3], in_=ss[q][:, 1:2]), r=[bss[q]], w=[bss[q]])
            yield
            P.op("dve", lambda e: e.scalar_tensor_tensor(out=hb[q][:], in0=xt[q][:], scalar=ss[q][:, 2:3], in1=gt[:], op0=ALU.mult, op1=ALU.mult),
                 r=[bx[q], bss[q], bg], w=[bh[q]])
            yield
            for k in range(8):
                P.op("pe", lambda e, k=k: e.transpose(out=ptr[:, k, :], in_=hb[q][:, k * 128:(k + 1) * 128], identity=ident[:]),
                     r=[bh[q], b_ident], **({"w": [bptr]} if k == 0 else {"pw": [bptr]}))
            yield
            P.op("act", lambda e: e.activation(out=dst_ap, in_=ptr[:], func=ACTF.Copy), r=[bptr], **{dst_w: [bdst]})
            yield

        wh = [P.sb("wh", [128, 8, 128], BF16) for _ in range(2)]; bwh = [Buf(), Buf()]
        psz = [P.ps("psz", [128, 512]) for _ in range(2)]; bpsz = [Buf(), Buf()]
        R = [P.sb("R", [128, SEG], F32) for _ in range(2)]
        bRk = [[Buf() for _ in range(9)] for _ in range(2)]
        Rb = [P.sb("Rb", [128, SEG], BF16) for _ in range(2)]; bRb = [Buf(), Buf()]
        psq = [P.ps("psq", [128, 512]) for _ in range(4)]; bpsq = [Buf() for _ in range(4)]
        sq = P.sb("sq", [128, 512], F32); bsq = Buf()
        ssq = P.sb("ssq", [128, 24], F32); bssq = Buf()
        qn = P.sb("qn", [128, 512], F32); bqn = Buf()
        qnb2 = [[P.sb("qnb", [128, 512], BF16) for _ in range(2)] for _ in range(2)]
        bqnb2 = [[Buf(), Buf()], [Buf(), Buf()]]
        ptq = P.ps("ptq", [128, 4, 128], BF16); bptq = Buf()
        QTo = [P.sb("QTo", [128, 4, 256], BF16) for _ in range(2)]; bQTo = [Buf(), Buf()]
        KTo = [P.sb("KTo", [128, 4, 256], BF16) for _ in range(2)]; bKTo = [Buf(), Buf()]
        Vo = [P.sb("Vo", [128, 2, 8, 65], BF16) for _ in range(2)]; bVo = [Buf(), Buf()]
        Go = [P.sb("Go", [128, 2, 512], BF16) for _ in range(2)]; bGo = [Buf(), Buf()]
        for q in range(2):
            P.op("pool", lambda e, q=q: e.memset(Vo[q][:], 1.0), w=[bVo[q]])

        blocks = [(o, min(510, SEG - o)) for o in range(0, SEG, 510)]
        wcnt = {"i": 0}
        for s in range(2):
            base = s * SEG
            run_pipelined([norm_tile(base + t * 128, hT[:, :, 1 + t * 128:1 + (t + 1) * 128], bhT, "w" if t == 0 else "pw")
                           for t in range(32)], 2)
            if s == 0:
                P.op("pool", lambda e: e.memset(hT[:, :, 0:1], 0.0), pw=[bhT])
                run_pipelined([norm_tile(SEG, edge[:], bedge, "w")], 1)
                P.op("dve", lambda e: e.tensor_scalar(out=hT[:, :, SEG + 1:SEG + 2], in0=edge[:, :, 0:1], scalar1=mflag[:, 0:1], scalar2=None,
                                                      op0=ALU.mult), r=[bedge, b_mflag], pw=[bhT])
            else:
                P.op("pool", lambda e: e.memset(hT[:, :, SEG + 1:SEG + 2], 0.0), pw=[bhT])
                run_pipelined([norm_tile(SEG - 128, edge[:], bedge, "w")], 1)
                P.op("dve", lambda e: e.tensor_scalar(out=hT[:, :, 0:1], in0=edge[:, :, 127:128], scalar1=mflag[:, 0:1], scalar2=None,
                                                      op0=ALU.mult), r=[bedge, b_mflag], pw=[bhT])
            for j in (range(4) if 'hy' in build.parts else ()):
                for kind, ch in (("x1", 4 + j), ("vv", 8 + j), ("x0", j), ("g", 12 + j)):
                    wq = wcnt["i"] % 2
                    wcnt["i"] += 1
                    P.dma("sp", wst[wq][:], wv[:, :, ch * 128:(ch + 1) * 128], w=[bwst[wq]])
                    P.op("act", lambda e, wq=wq: e.activation(out=wh[wq][:], in_=wst[wq][:], func=ACTF.Copy), r=[bwst[wq]], w=[bwh[wq]])
                    rq = 0 if kind == "x1" else 1
                    for bi, (o0, wd) in enumerate(blocks):
                        pq = bi % 2
                        for k in range(8):
                            P.op("pe", lambda e, pq=pq, k=k, wq=wq, o0=o0, wd=wd: e.matmul(
                                psz[pq][:, 0:wd + 2], lhsT=wh[wq][:, k, :], rhs=hT[:, k, o0:o0 + wd + 2], start=(k == 0), stop=(k == 7)),
                                r=[bwh[wq], bhT], **({"w": [bpsz[pq]]} if k == 0 else {"pw": [bpsz[pq]]}))
                        bk_ = bRk[rq][bi]
                        fw = {"w": [bRb[0]]} if bi == 0 else {"pw": [bRb[0]]}
                        if kind == "g":
                            act(Rb[1][:, o0:o0 + wd], psz[pq][:, 1:wd + 1], ACTF.Silu, r=[bpsz[pq]], **({"w": [bRb[1]]} if bi == 0 else {"pw": [bRb[1]]}))
                        else:
                            act(R[rq][:, o0:o0 + wd], psz[pq][:, 1:wd + 1], ACTF.Identity, r=[bpsz[pq], bsm],
                                scale=cw[:, ch, 1:2], bias=cb[:, ch:ch + 1], w=[bk_])
                            P.op("dve", lambda e, pq=pq, o0=o0, wd=wd, rq=rq, ch=ch: e.scalar_tensor_tensor(
                                out=R[rq][:, o0:o0 + wd], in0=psz[pq][:, 0:wd], scalar=cw[:, ch, 0:1], in1=R[rq][:, o0:o0 + wd],
                                op0=ALU.mult, op1=ALU.add), r=[bpsz[pq], bsm, bk_], pw=[bk_])
                            if kind == "x0":
                                P.op("dve", lambda e, pq=pq, o0=o0, wd=wd, rq=rq, ch=ch: e.scalar_tensor_tensor(
                                    out=Rb[0][:, o0:o0 + wd], in0=psz[pq][:, 2:wd + 2], scalar=cw[:, ch, 2:3], in1=R[rq][:, o0:o0 + wd],
                                    op0=ALU.mult, op1=ALU.add), r=[bpsz[pq], bsm, bk_], **fw)
                            else:
                                P.op("dve", lambda e, pq=pq, o0=o0, wd=wd, rq=rq, ch=ch: e.scalar_tensor_tensor(
                                    out=R[rq][:, o0:o0 + wd], in0=psz[pq][:, 2:wd + 2], scalar=cw[:, ch, 2:3], in1=R[rq][:, o0:o0 + wd],
                                    op0=ALU.mult, op1=ALU.add), r=[bpsz[pq], bsm, bk_], pw=[bk_])
                            if kind == "vv":
                                P.op("pool", lambda e, o0=o0, wd=wd: e.tensor_tensor(out=Rb[0][:, o0:o0 + wd], in0=R[0][:, o0:o0 + wd],
                                                                                    in1=R[1][:, o0:o0 + wd], op=ALU.mult),
                                     r=[bRk[0][bi], bRk[1][bi]], **fw)
                    rows = slice(j * 128, (j + 1) * 128)
                    colsl = slice(base, base + SEG)
                    if kind == "vv":
                        P.dma(SQ, uD[rows, colsl], Rb[0][:], r=[bRb[0]], pw=[bD["uD"]])
                    elif kind == "x0":
                        P.dma(SQ, x0D[rows, colsl], Rb[0][:], r=[bRb[0]], pw=[bD["x0D"]])
                    elif kind == "g":
                        P.dma(SQ, gD[rows, colsl], Rb[1][:], r=[bRb[1]], pw=[bD["gD"]])
            def partA(t):
                tok = slice(1 + t * 128, 1 + (t + 1) * 128)
                tg = s * 32 + t
                ob = (tg // 2) % 2
                oi = t % 2
                par = t % 2
                for m in (2, 3, 0, 1):
                    for k in range(8):
                        P.op("pe", lambda e, m=m, k=k, tok=tok: e.matmul(psq[m][:], lhsT=hT[:, k, tok], rhs=wA[:, k, m * 512:(m + 1) * 512],
                                                                        start=(k == 0), stop=(k == 7)),
                             r=[bhT, bwA], **({"w": [bpsq[m]]} if k == 0 else {"pw": [bpsq[m]]}))
                    if m == 2:
                        act(Vo[ob][:, oi, :, 0:64], psq[2][:].rearrange("p (h d) -> p h d", d=64), ACTF.Copy, r=[bpsq[2]], pw=[bVo[ob]])
                    elif m == 3:
                        act(Go[ob][:, oi, :], psq[3][:], ACTF.Silu, r=[bpsq[3]], **({"w": [bGo[ob]]} if oi == 0 else {"pw": [bGo[ob]]}))
                for m, gsb in ((0, qg), (1, kg)):
                    qb, bqb = qnb2[par][m], bqnb2[par][m]
                    act(sq[:], psq[m][:], ACTF.Square, r=[bpsq[m]], w=[bsq])
                    P.op("dve", lambda e: e.tensor_reduce(out=ssq[:, 0:8], in_=sq[:].rearrange("p (h d) -> p h d", d=64), axis=AX.X, op=ALU.add),
                         r=[bsq], w=[bssq])
                    rstd_from_ss(ssq[:, 0:8], ssq[:, 8:16], ssq[:, 16:24], 1.0 / 64, bssq)
                    P.op("dve", lambda e, m=m: e.tensor_tensor(out=qn[:].rearrange("p (h d) -> p h d", d=64),
                                                               in0=psq[m][:].rearrange("p (h d) -> p h d", d=64),
                                                               in1=ssq[:, 16:24].unsqueeze(2).to_broadcast([128, 8, 64]), op=ALU.mult),
                         r=[bpsq[m], bssq], w=[bqn])
                    P.op("pool", lambda e, gsb=gsb, qb=qb: e.tensor_tensor(out=qb[:].rearrange("p (h d) -> p h d", d=64),
                                                                           in0=qn[:].rearrange("p (h d) -> p h d", d=64),
                                                                           in1=gsb[:].unsqueeze(1).to_broadcast([128, 8, 64]), op=ALU.mult),
                         r=[bqn, bsm], w=[bqb])

            def partB(t):
                tg = s * 32 + t
                ob = (tg // 2) % 2
                oi = t % 2
                par = t % 2
                for m, dst, bdst in ((0, QTo, bQTo), (1, KTo, bKTo)):
                    qb, bqb = qnb2[par][m], bqnb2[par][m]
                    for hp in range(4):
                        P.op("pe", lambda e, hp=hp, qb=qb: e.transpose(out=ptq[:, hp, :], in_=qb[:, hp * 128:(hp + 1) * 128], identity=ident[:]),
                             r=[bqb, b_ident], **({"w": [bptq]} if hp == 0 else {"pw": [bptq]}))
                    P.op("dve", lambda e, dst=dst, ob=ob, oi=oi: e.tensor_copy(out=dst[ob][:, :, oi * 128:(oi + 1) * 128], in_=ptq[:]),
                         r=[bptq], **({"w": [bdst[ob]]} if oi == 0 else {"pw": [bdst[ob]]}))
                if oi == 1:
                    t0 = (tg - 1) * 128
                    P.dma(SQ, QTD[:, :, t0:t0 + 256].rearrange("a p t -> p a t"), QTo[ob][:], r=[bQTo[ob]], pw=[bD["QTD"]])
                    P.dma(SQ, KTD[:, :, t0:t0 + 256].rearrange("a p t -> p a t"), KTo[ob][:], r=[bKTo[ob]], pw=[bD["KTD"]])
                    P.dma(SQ, VD[tg - 1:tg + 1].rearrange("c p f -> p c f"), Vo[ob][:].rearrange("p c h f -> p c (h f)"), r=[bVo[ob]], pw=[bD["VD"]])
                    P.dma(SQ, GD[tg - 1:tg + 1].rearrange("c p f -> p c f"), Go[ob][:], r=[bGo[ob]], pw=[bD["GD"]])

            if 'att' in build.parts:
                for t in range(33):
                    if t < 32:
                        partA(t)
                    if t >= 1:
                        partB(t - 1)
        P.barrier()
        P.release()

    def stage_attn(l):
        P.mark()
        KT = P.sb("KT", [128, 4, T], BF16); bKT = [Buf() for _ in range(8)]
        V = P.sb("V", [128, 64, 520], BF16); bV = [Buf() for _ in range(8)]
        ona = P.sb("ona", [128, 512], F32); bona = Buf()
        P.dma("sp", ona[:], D["on_att"][l], w=[bona])
        rp = P.sb("rp", [128, 7, 8, 128], F32); brp = Buf()
        for d in range(7):
            P.dma("sp", rp[:, d], D["rpbT"][l, d], pw=[brp])
        mint = P.sb("mint", [128, 5, 128], BF16); bmint = Buf()
        P.dma("sp", mint[:], D["mask_int"].rearrange("d p q -> p d q"), w=[bmint])
        bint = P.sb("bint", [128, 5, 8, 128], BF16); bbint = Buf()
        kv_state = {"n": 0}

        def ensure_kv(upto):
            while kv_state["n"] <= min(7, upto):
                tq = kv_state["n"]
                P.dma("sp", KT[:, :, tq * 1024:(tq + 1) * 1024], KTD[:, :, tq * 1024:(tq + 1) * 1024].rearrange("a p t -> p a t"),
                      r=[bD["KTD"]], w=[bKT[tq]])
                P.dma("sp", V[:, tq * 8:(tq + 1) * 8, :], VD[tq * 8:(tq + 1) * 8].rearrange("c p f -> p c f"), r=[bD["VD"]], w=[bV[tq]])
                kv_state["n"] += 1
        ensure_kv(0)
        for di, d in enumerate(INTERIOR):
            P.op("dve", lambda e, di=di, d=d: e.scalar_tensor_tensor(out=bint[:, di], in0=rp[:, d + 3], scalar=8.0,
                                                                      in1=mint[:, di, :].unsqueeze(1).to_broadcast([128, 8, 128]),
                                                                      op0=ALU.mult, op1=ALU.add), r=[brp, bmint], pw=[bbint])
        msp = P.sb("msp", [128, 6, 128], BF16); bmsp = Buf()
        bsp = P.sb("bsp", [128, 6, 8, 128], BF16); bbsp = Buf()
        QT = [P.sb("QT", [128, 4, 2, 128], BF16) for _ in range(2)]; bQT = [Buf(), Buf()]
        for q in range(2):
            P.op("pool", lambda e, q=q: e.memset(QT[q][:], 0.0), w=[bQT[q]])
        Gt = [P.sb("Gt", [128, 512], BF16) for _ in range(2)]; bGt = [Buf(), Buf()]
        NR = 5
        psS = [P.ps("psS", [128, 2, 2, 128]) for _ in range(NR)]; bpsS = [Buf() for _ in range(NR)]
        NRP = 6
        Pt = [P.sb("Pt", [128, 2, 2, 128], BF16) for _ in range(NRP)]; bPt = [Buf() for _ in range(NRP)]
        pring = {"i": 0}
        ring = {"i": 0}
        psO = [P.ps("psO", [128, 4, 65]) for _ in range(2)]; bpsO = [Buf(), Buf()]
        den = P.sb("den", [128, 16], F32); bden = Buf()
        yat = P.sb("yat", [128, 512], F32); byat = Buf()
        junk = P.sb("junk2", [128, 512], BF16); bjunk = Buf()
        ssa = P.sb("ssa", [128, 4], F32); bssa = Buf()
        ya2 = P.sb("ya2", [128, 512], F32); bya2 = Buf()
        yab = P.sb("yab", [128, 512], BF16); byab = Buf()
        pta = P.ps("pta", [128, 4, 128], BF16); bpta = Buf()
        ATo = [P.sb("ATo", [128, 4, 256], BF16) for _ in range(2)]; bATo = [Buf(), Buf()]
        sp_state = {"off": 0}

        def setup(p):
            q = p % 2
            if p in SPECIAL:
                dl = SPECIAL[p]
                nd = len(dl)
                so_ = sp_state["off"]
                P.dma("sp", msp[:, 0:nd, :], D["mask_sp"][so_:so_ + nd].rearrange("d p q -> p d q"), w=[bmsp])
                sp_state["off"] += nd
                for di, d in enumerate(dl):
                    P.op("dve", lambda e, di=di, d=d: e.scalar_tensor_tensor(out=bsp[:, di], in0=rp[:, d + 3], scalar=8.0,
                                                                              in1=msp[:, di, :].unsqueeze(1).to_broadcast([128, 8, 128]),
                                                                              op0=ALU.mult, op1=ALU.add),
                         r=[brp, bmsp], **({"w": [bbsp]} if di == 0 else {"pw": [bbsp]}))
                btile, bbt = bsp, bbsp
            else:
                dl = INTERIOR
                nd = 5
                btile, bbt = bint, bbint
            qsrc = QTD[:, :, p * 128:(p + 1) * 128].rearrange("a p t -> p a t")
            P.dma("sp", QT[q][0:64, :, 0, :], qsrc[0:64], r=[bD["QTD"]], w=[bQT[q]])
            P.dma("sp", QT[q][64:128, :, 1, :], qsrc[64:128], r=[bD["QTD"]], pw=[bQT[q]])
            P.dma("sp", Gt[q][:], GD[p], r=[bD["GD"]], w=[bGt[q]])
            return dict(p=p, q=q, dl=dl, nd=nd, btile=btile, bbt=bbt)

        def partA(cx, hp):
            p, q, dl, nd, btile, bbt = cx["p"], cx["q"], cx["dl"], cx["nd"], cx["btile"], cx["bbt"]
            banks = []
            for j in range((nd + 1) // 2):
                ri = ring["i"] % NR
                ring["i"] += 1
                pi_ = pring["i"] % NRP
                pring["i"] += 1
                banks.append(pi_)
                ndj = min(2, nd - 2 * j)
                for dj in range(ndj):
                    di = 2 * j + dj
                    c = p + dl[di]
                    so = psS[ri][:, dj, :, :]
                    P.op("pe", lambda e, so=so, hp=hp, c=c, q=q: e.matmul(
                        so, lhsT=KT[:, hp, c * 128:(c + 1) * 128], rhs=QT[q][:, hp, :, :], start=True, stop=False),
                        r=[bKT[c // 8], bQT[q]], **({"w": [bpsS[ri]]} if dj == 0 else {"pw": [bpsS[ri]]}))
                    P.op("pe", lambda e, so=so, di=di, hp=hp, btile=btile: e.matmul(
                        so, lhsT=ident[:], rhs=btile[:, di, 2 * hp:2 * hp + 2, :], start=False, stop=True),
                        r=[b_ident, bbt], pw=[bpsS[ri]])
                act(Pt[pi_][:, 0:ndj], psS[ri][:, 0:ndj], ACTF.Exp, r=[bpsS[ri]], w=[bPt[pi_]], scale=0.125)
            return banks

        def partB(cx, hp, banks):
            p, dl, nd = cx["p"], cx["dl"], cx["nd"]
            for hh in range(2):
                h = 2 * hp + hh
                oq = h // 4
                for di in range(nd):
                    ri = banks[di // 2]
                    dj = di % 2
                    c = p + dl[di]
                    P.op("pe", lambda e, oq=oq, h=h, hh=hh, di=di, dj=dj, c=c, ri=ri, nd=nd: e.matmul(
                        psO[oq][:, h % 4, :], lhsT=Pt[ri][:, dj, hh, :], rhs=V[:, c, h * 65:(h + 1) * 65], start=(di == 0), stop=(di == nd - 1)),
                        r=[bPt[ri], bV[c // 8]], **({"w": [bpsO[oq]]} if (h % 4 == 0 and di == 0) else {"pw": [bpsO[oq]]}))

        def tail(cx):
            p, q = cx["p"], cx["q"]
            for oq in range(2):
                P.op("dve", lambda e, oq=oq: e.reciprocal(out=den[:, oq * 4:(oq + 1) * 4], in_=psO[oq][:, :, 64]), r=[bpsO[oq]],
                     **({"w": [bden]} if oq == 0 else {"pw": [bden]}))
            for oq in range(2):
                P.op("dve", lambda e, oq=oq: e.tensor_tensor(out=yat[:, oq * 256:(oq + 1) * 256].rearrange("p (h d) -> p h d", d=64),
                                                             in0=psO[oq][:, :, 0:64],
                                                             in1=den[:, oq * 4:(oq + 1) * 4].unsqueeze(2).to_broadcast([128, 4, 64]), op=ALU.mult),
                     r=[bpsO[oq], bden], **({"w": [byat]} if oq == 0 else {"pw": [byat]}))
            P.op("pool", lambda e: e.tensor_tensor(out=junk[:], in0=yat[:], in1=yat[:], op=ALU.mult), r=[byat], w=[bjunk])
            P.op("dve", lambda e: e.tensor_reduce(out=ssa[:, 0:1], in_=junk[:], axis=AX.X, op=ALU.add), r=[bjunk], w=[bssa])
            P.op("act", lambda e: e.activation(out=ssa[:, 1:2], in_=ssa[:, 0:1], func=ACTF.Sqrt, scale=1.0 / 512, bias=epsc[:]),
                 r=[bssa, b_eps], w=[bssa])
            P.op("dve", lambda e: e.reciprocal(out=ssa[:, 2:3], in_=ssa[:, 1:2]), r=[bssa], w=[bssa])
            P.op("pool", lambda e: e.tensor_tensor(out=ya2[:], in0=yat[:], in1=ona[:], op=ALU.mult), r=[byat, bona], w=[bya2])
            P.op("dve", lambda e: e.scalar_tensor_tensor(out=yab[:], in0=ya2[:], scalar=ssa[:, 2:3], in1=Gt[q][:], op0=ALU.mult, op1=ALU.mult),
                 r=[bya2, bssa, bGt[q]], w=[byab])

        def tail2(cx):
            p = cx["p"]
            for k in range(4):
                P.op("pe", lambda e, k=k: e.transpose(out=pta[:, k, :], in_=yab[:, k * 128:(k + 1) * 128], identity=ident[:]),
                     r=[byab, b_ident], **({"w": [bpta]} if k == 0 else {"pw": [bpta]}))
            ob = (p // 2) % 2
            oi = p % 2
            P.op("dve", lambda e, ob=ob, oi=oi: e.tensor_copy(out=ATo[ob][:, :, oi * 128:(oi + 1) * 128], in_=pta[:]), r=[bpta],
                 **({"w": [bATo[ob]]} if oi == 0 else {"pw": [bATo[ob]]}))
            if oi == 1:
                t0 = (p - 1) * 128
                P.dma(SQ, ATD[:, :, t0:t0 + 256].rearrange("a p t -> p a t"), ATo[ob][:], r=[bATo[ob]], pw=[bD["ATD"]])

        units = [(p, hp) for p in range(64) for hp in range(4)]
        cxs = {}
        prev = None
        for u in units + [None]:
            cur = None
            if u is not None:
                p, hp = u
                if hp == 0:
                    cxs[p] = setup(p)
                    ensure_kv((p + 3) // 8 + 1)
                cur = (p, hp, partA(cxs[p], hp))
            if prev is not None:
                pp, php, pbanks = prev
                partB(cxs[pp], php, pbanks)
                if php == 3:
                    tail(cxs[pp])
                if php == 1 and pp >= 1:
                    tail2(cxs[pp - 1])
            prev = cur
        tail2(cxs[63])
        P.barrier()
        P.release()

    def stage_out(l, xsrc, bxsrc, xdst, bxdst):
        P.mark()
        wo = P.sb("wo", [128, 8, 1024], BF16); bwo = Buf()
        wst = [P.sb("wost", [128, 8, 128], F32) for _ in range(2)]; bwst = [Buf(), Buf()]
        wv = D["w_out"][l].rearrange("(k p) c -> p k c", p=128)
        for j in range(8):
            q = j % 2
            P.dma("sp", wst[q][:], wv[:, :, j * 128:(j + 1) * 128], w=[bwst[q]])
            P.op("act", lambda e, q=q, j=j: e.activation(out=wo[:, :, j * 128:(j + 1) * 128], in_=wst[q][:], func=ACTF.Copy), r=[bwst[q]], pw=[bwo])
        sm = P.sb("smo", [128, 8], F32); bsm = Buf()
        P.dma("sp", sm[:, 0:4], D["hy_bias"][l], pw=[bsm])
        P.dma("sp", sm[:, 4:8], D["on_hy"][l], pw=[bsm])
        ones = P.sb("ones", [128, 128], BF16); bones = Buf()
        P.op("pool", lambda e: e.memset(ones[:], 1.0), w=[bones])
        NBK = 16
        x0t = [P.sb("x0t", [128, 4, 512], BF16) for _ in range(2)]; bx0 = [Buf(), Buf()]
        ut = [P.sb("ut", [128, 4, 512], BF16) for _ in range(2)]; but = [Buf(), Buf()]
        gtt = [P.sb("gtt", [128, 4, 512], BF16) for _ in range(2)]; bgt = [Buf(), Buf()]
        yt = [P.sb("yt", [128, 4, 512], F32) for _ in range(2)]; byt = [Buf(), Buf()]
        AaT = [P.sb("AaT", [128, 4, 512], BF16) for _ in range(3)]; bAa = [Buf(), Buf(), Buf()]
        tmp = P.sb("tmpo", [128, 4, 512], F32); btmp = Buf()
        yraw = P.sb("yraw", [128, 4, 512], F32); byraw = Buf()
        sqh = P.sb("sqh", [128, 4, 512], BF16); bsqh = Buf()
        grs = P.sb("grs", [128, 4, 512], F32); bgrs = Buf()
        AhT = [P.sb("AhT", [128, 4, 512], BF16) for _ in range(2)]; bAh = [Buf(), Buf()]
        pss = P.ps("pss", [128, 512]); bpss = Buf()
        rsb = P.sb("rsb", [128, 512], F32); brsb = Buf()
        rsb2 = P.sb("rsb2", [128, 512], F32); brsb2 = Buf()
        psP = [P.ps("psP", [128, 512]) for _ in range(4)]; bpsP = [Buf() for _ in range(4)]
        xt = [P.sb("xto", [128, 4, 1024], F32) for _ in range(3)]; bxt = [Buf(), Buf(), Buf()]
        xo = [P.sb("xo", [128, 4, 1024], F32) for _ in range(2)]; bxo = [Buf(), Buf()]
        cm = lambda ap, t0: ap[:, t0:t0 + 512].rearrange("(k p) t -> p k t", p=128)

        def prepA(b):
            q = b % 2
            q3 = b % 3
            t0 = b * 512
            P.dma("sp", x0t[q][:], cm(x0D, t0), r=[bD["x0D"]], w=[bx0[q]])
            P.dma("sp", ut[q][:], cm(uD, t0), r=[bD["uD"]], w=[but[q]])
            P.dma("sp", gtt[q][:], cm(gD, t0), r=[bD["gD"]], w=[bgt[q]])
            P.dma("sp", yt[q][:], cm(yD, t0), r=[bD["yD"]], w=[byt[q]])
            P.dma("sp", AaT[q3][:], ATD[:, :, t0:t0 + 512].rearrange("a p t -> p a t"), r=[bD["ATD"]], w=[bAa[q3]])
            P.dma("sp", xt[q3][:], xsrc[t0:t0 + 512, :].rearrange("(t p) d -> p t d", p=128), r=[bxsrc], w=[bxt[q3]])
            for k in range(4):
                P.op("dve", lambda e, k=k: e.scalar_tensor_tensor(out=tmp[:, k, :], in0=ut[q][:, k, :], scalar=sm[:, k:k + 1], in1=yt[q][:, k, :],
                                                                  op0=ALU.mult, op1=ALU.add), r=[but[q], byt[q], bsm],
                     **({"w": [btmp]} if k == 0 else {"pw": [btmp]}))
            P.op("pool", lambda e: e.tensor_tensor(out=yraw[:], in0=tmp[:], in1=x0t[q][:], op=ALU.mult), r=[btmp, bx0[q]], w=[byraw])
            act(sqh[:], yraw[:], ACTF.Square, r=[byraw], w=[bsqh])

        def prepB(b):
            q = b % 2
            for k in range(4):
                P.op("pe", lambda e, k=k: e.matmul(pss[:], lhsT=ones[:], rhs=sqh[:, k, :], start=(k == 0), stop=(k == 3)),
                     r=[bsqh, bones], **({"w": [bpss]} if k == 0 else {"pw": [bpss]}))
            P.op("act", lambda e: e.activation(out=rsb[:], in_=pss[:], func=ACTF.Sqrt, scale=1.0 / 512, bias=epsc[:]),
                 r=[bpss, b_eps], w=[brsb])
            P.op("dve", lambda e: e.reciprocal(out=rsb2[:], in_=rsb[:]), r=[brsb], w=[brsb2])
            P.op("pool", lambda e: e.tensor_tensor(out=grs[:], in0=gtt[q][:], in1=rsb2[:].unsqueeze(1).to_broadcast([128, 4, 512]), op=ALU.mult),
                 r=[bgt[q], brsb2], w=[bgrs])
            for k in range(4):
                P.op("dve", lambda e, k=k: e.scalar_tensor_tensor(out=AhT[q][:, k, :], in0=yraw[:, k, :], scalar=sm[:, 4 + k:5 + k], in1=grs[:, k, :],
                                                                  op0=ALU.mult, op1=ALU.mult), r=[byraw, bgrs, bsm],
                     **({"w": [bAh[q]]} if k == 0 else {"pw": [bAh[q]]}))

        def tiles(b):
            q = b % 2
            q3 = b % 3
            for tt_ in range(4):
                tg = b * 4 + tt_
                xq = tg % 2
                ts_ = slice(tt_ * 128, (tt_ + 1) * 128)
                for half in range(2):
                    pi = xq * 2 + half
                    for k in range(8):
                        src, bs = (AhT[q], bAh[q]) if k < 4 else (AaT[q3], bAa[q3])
                        P.op("pe", lambda e, src=src, k=k, pi=pi, half=half, ts_=ts_: e.matmul(
                            psP[pi][:], lhsT=src[:, k % 4, ts_], rhs=wo[:, k, half * 512:(half + 1) * 512], start=(k == 0), stop=(k == 7)),
                            r=[bs, bwo], **({"w": [bpsP[pi]]} if k == 0 else {"pw": [bpsP[pi]]}))
                for half in range(2):
                    pi = xq * 2 + half
                    hs = slice(half * 512, (half + 1) * 512)
                    P.op("dve", lambda e, pi=pi, hs=hs, tt_=tt_: e.tensor_tensor(out=xo[q][:, tt_, hs], in0=psP[pi][:], in1=xt[q3][:, tt_, hs], op=ALU.add),
                         r=[bpsP[pi], bxt[q3]], **({"w": [bxo[q]]} if (half == 0 and tt_ == 0) else {"pw": [bxo[q]]}))
            P.dma(SQ, xdst[b * 512:(b + 1) * 512, :].rearrange("(t p) d -> p t d", p=128), xo[q][:], r=[bxo[q]], pw=[bxdst])

        prepA(0)
        prepB(0)
        prepA(1)
        for b in range(NBK):
            if b + 1 < NBK:
                prepB(b + 1)
            if b + 2 < NBK:
                prepA(b + 2)
            tiles(b)
        P.barrier()
        P.release()

    stages = build.stages
    for l in range(DEPTH):
        xsrc, bxs = (D["x"], Buf()) if l == 0 else (x1, bD["x1"])
        xdst, bxd = (x1, bD["x1"]) if l == 0 else (y_out, bD["y"])
        if "filter" in stages:
            stage_filter(l)
        if "proj" in stages:
            stage_proj(l, xsrc, bxs)
        if "ffft" in stages or "conv" in stages:
            P.mark()
            C, bc = fft_consts()
            if "ffft" in stages:
                stage_filter_fft(l, C, bc)
            if "conv" in stages:
                stage_conv(l, C, bc)
            P.release()
        if "attn" in stages:
            stage_attn(l)
        if "out" in stages:
            stage_out(l, xsrc, bxs, xdst, bxd)
        if build.one_layer:
            break
    P.barrier()
    P.emit()
    return nc


build.stages = ("filter", "proj", "ffft", "conv", "attn", "out")
build.one_layer = False
build.parts = ('hy', 'att')


def _dup(a):
    return np.ascontiguousarray(np.concatenate([a, a], 1)[:, :, None])


def _pad_w1(w):
    z = np.zeros_like(w)
    return np.ascontiguousarray(np.stack([np.concatenate([w, z], 2), np.concatenate([z, w], 2)], 1))


def _blockdiag(w):
    out = np.zeros((w.shape[0], 128, 128), np.float32)
    out[:, :64, :64] = w
    out[:, 64:, 64:] = w
    return out


def _pad_w3(w):
    out = np.zeros((w.shape[0], 128, 1024), np.float32)
    out[:, :64, :512] = w[:, :, :512]
    out[:, 64:, 512:] = w[:, :, 512:]
    return out

def make_in_maps(inp):
    f = lambda a: np.ascontiguousarray(np.asarray(a, dtype=np.float32))
    xp, xs = f(inp["x_prompt"]), f(inp["x_sample"])
    slots = [xp[0], xp[1], xp[2], xp[3], xs[0:2].reshape(T, 1024), xs[2:4].reshape(T, 1024)]
    slots += [slots[4], slots[5]]
    isp = [True] * 4 + [False] * 4
    cP, cS = _consts(True), _consts(False)
    bc128 = lambda a: np.ascontiguousarray(np.broadcast_to(a[:, None, :], (a.shape[0], 128, a.shape[1])))
    chan = lambda a, k: np.ascontiguousarray(a.reshape(a.shape[0], k, 128).transpose(0, 2, 1))
    cw = f(inp["conv_w"])
    shared = {
        "norm_g": bc128(f(inp["norm_g"])),
        "w_in": f(inp["w_in"]),
        "conv_w": np.ascontiguousarray(cw.reshape(DEPTH, 3, 12, 128).transpose(0, 3, 2, 1)),
        "conv_b": chan(f(inp["conv_b"]), 12),
        "f_w1": _pad_w1(f(inp["f_w1"])),
        "f_b1": _dup(f(inp["f_b1"])),
        "f_fr1": _dup(f(inp["f_fr1"])),
        "f_w2": _blockdiag(f(inp["f_w2"])),
        "f_b2": _dup(f(inp["f_b2"])),
        "f_fr2": _dup(f(inp["f_fr2"])),
        "f_w3": _pad_w3(f(inp["f_w3"])),
        "hy_bias": chan(f(inp["hy_bias"]), 4),
        "on_hy": chan(f(inp["on_hy"]), 4),
        "qn_g": bc128(f(inp["qn_g"])),
        "kn_g": bc128(f(inp["kn_g"])),
        "on_att": bc128(f(inp["on_att"])),
        "rpbT": np.stack([_rpb_gather(f(inp["rpb"])[l]) for l in range(DEPTH)]),
        "w_out": f(inp["w_out"]),
    }
    maps = []
    for c in range(8):
        m = dict(shared)
        m.update(cP if isp[c] else cS)
        m["x"] = np.ascontiguousarray(slots[c])
        maps.append(m)
    return maps


_NC = {}


def kernel(**inputs):
    if "nc" not in _NC:
        _NC["nc"] = build()
    maps = make_in_maps(inputs)
    res = run_bass_kernel_spmd(_NC["nc"], maps, core_ids=list(range(8)))
    ys = [np.asarray(res.results[c]["y"], dtype=np.float32) for c in range(6)]
    y_prompt = np.stack(ys[0:4])
    y_sample = np.concatenate([ys[4].reshape(2, SEG, 1024), ys[5].reshape(2, SEG, 1024)], 0)
    return (y_prompt, y_sample)
```

```python
import math
import numpy as np
import ml_dtypes
import concourse.bass as bass
import concourse.mybir as mybir
from concourse.bass_utils import run_bass_kernel_spmd

F32 = mybir.dt.float32
BF16 = mybir.dt.bfloat16
ALU = mybir.AluOpType
ACTF = mybir.ActivationFunctionType
AX = mybir.AxisListType
NPBF = ml_dtypes.bfloat16

NDS = 56
SQ = "act"
SEMB = 30000

T = 8192
SEG = 4096
NRING = 16384
EPS = 1e-6
PI = math.pi
DEPTH = 2
MASKV = -80.0
SPECIAL = {0: [0, 1, 2, 3], 1: [-1, 0, 1, 2], 30: [-2, -1, 0, 1, 2], 31: [-3, -2, -1, 0, 1, 2],
           32: [-2, -1, 0, 1, 2, 3], 33: [-2, -1, 0, 1, 2], 62: [-2, -1, 0, 1], 63: [-3, -2, -1, 0]}
SPEC_LIST = sorted(SPECIAL)
INTERIOR = [-2, -1, 0, 1, 2]


class Buf:
    __slots__ = ("writers", "readers")

    def __init__(self):
        self.writers = {}
        self.readers = {}


class Prog:
    def __init__(self, nc):
        self.nc = nc
        self.names = ["pe", "act", "dve", "pool", "sp"]
        self.lists = {e: [] for e in self.names}
        self.count = {e: 0 for e in self.names}
        self.seen = {e: {} for e in self.names}
        self.dma_next = 0
        self.dma_tgt = [0] * NDS
        self.sems = {}
        self._stack = []
        self._marks = []
        self.uid = 0

    def mark(self):
        self._marks.append(len(self._stack))

    def release(self):
        m = self._marks.pop()
        while len(self._stack) > m:
            self._stack.pop().__exit__(None, None, None)

    def sb(self, name, shape, dt):
        self.uid += 1
        cm = self.nc.sbuf_tensor("%s_%d" % (name, self.uid), list(shape), dt)
        t = cm.__enter__()
        self._stack.append(cm)
        return t

    def ps(self, name, shape, dt=F32):
        self.uid += 1
        cm = self.nc.psum_tensor("%s_%d" % (name, self.uid), list(shape), dt)
        t = cm.__enter__()
        self._stack.append(cm)
        return t

    def _sem(self, key):
        if key not in self.sems:
            self.sems[key] = self.nc.semaphore("s_%s_%s" % (key[0], key[1])).__enter__()
        return self.sems[key]

    def _collect(self, eng, r, w, pw):
        need = {}
        for b in r:
            for k, v in b.writers.items():
                if need.get(k, 0) < v:
                    need[k] = v
        for b in w:
            for d in (b.writers, b.readers):
                for k, v in d.items():
                    if need.get(k, 0) < v:
                        need[k] = v
        for b in pw:
            for k, v in b.readers.items():
                if need.get(k, 0) < v:
                    need[k] = v
        out = []
        seen = self.seen[eng]
        for k, v in need.items():
            if k == ("e", "pe") and eng == "pe":
                continue
            if seen.get(k, 0) >= v:
                continue
            seen[k] = v
            out.append((k, v))
        return out

    def _update(self, key, n, r, w, pw):
        for b in r:
            b.readers[key] = n
        for b in w:
            b.writers = {key: n}
            b.readers = {}
        for b in pw:
            b.writers[key] = n

    def op(self, eng, fn, r=(), w=(), pw=()):
        waits = self._collect(eng, r, w, pw)
        self.count[eng] += 1
        n = self.count[eng]
        key = ("e", eng)
        self.lists[eng].append((waits, fn, key, n))
        self._update(key, n, r, w, pw)

    def dma(self, q, out_ap, in_ap, r=(), w=(), pw=()):
        s = self.dma_next
        self.dma_next = (s + 1) % NDS
        waits = self._collect(q, r, w, pw)
        key = ("d", s)
        prev = self.dma_tgt[s]
        if prev and self.seen[q].get(key, 0) < prev:
            self.seen[q][key] = prev
            waits.append((key, prev))
        self.dma_tgt[s] += 16
        n = self.dma_tgt[s]
        self.lists[q].append((waits, lambda e: e.dma_start(out=out_ap, in_=in_ap), key, n))
        self._update(key, n, r, w, pw)

    def barrier(self):
        allk = {}
        for e in self.names:
            if self.count[e]:
                allk[("e", e)] = self.count[e]
        for s in range(NDS):
            if self.dma_tgt[s]:
                allk[("d", s)] = self.dma_tgt[s]
        for e in self.names:
            waits = []
            for k, v in allk.items():
                if k == ("e", e):
                    continue
                if self.seen[e].get(k, 0) >= v:
                    continue
                self.seen[e][k] = v
                waits.append((k, v))
            if waits:
                self.lists[e].append((waits, None, None, 0))

    def _semval(self, key, v):
        if key[0] == "d":
            return self._sem(key), v
        idx = self._sigidx[key[1]][v]
        c = (idx - 1) // SEMB
        return self._sem((key[1], c)), (idx - 1) % SEMB + 1

    def emit(self):
        nc = self.nc
        waited = {e: set() for e in self.names}
        for e in self.names:
            for (waits, fn, key, n) in self.lists[e]:
                for (k, v) in waits:
                    if k[0] == "e":
                        waited[k[1]].add(v)
        self._sigidx = {e: {n: i + 1 for i, n in enumerate(sorted(waited[e]))} for e in self.names}
        for e in self.names:
            for (waits, fn, key, n) in self.lists[e]:
                for (k, v) in waits:
                    self._semval(k, v)
                if key is not None and (key[0] == "d" or n in self._sigidx[key[1]]):
                    self._semval(key, n)
        with nc.Block() as block:
            regs = {"pe": block.tensor, "act": block.scalar, "dve": block.vector,
                    "pool": block.gpsimd, "sp": block.sync}
            for ename in self.names:
                def mk(lst):
                    def f(e):
                        for (waits, fn, key, n) in lst:
                            for (k, v) in waits:
                                s, val = self._semval(k, v)
                                e.wait_ge(s, val)
                            if fn is None:
                                continue
                            ins = fn(e)
                            if key[0] == "d":
                                s, val = self._semval(key, n)
                                ins.then_inc(s, 16)
                            elif n in self._sigidx[key[1]]:
                                s, val = self._semval(key, n)
                                ins.then_inc(s, 1)
                    return f
                regs[ename](mk(self.lists[ename]))


def _consts(is_prompt):
    c = {}
    L = 8192 if is_prompt else 4096
    n1map = np.arange(64) if is_prompt else np.where(np.arange(64) < 32, np.arange(64), np.arange(64) + 32)
    k1 = np.arange(65)
    ang = 2 * np.pi * np.outer(n1map, k1) / 128.0
    c["F1"] = np.concatenate([np.cos(ang), -np.sin(ang)], 1).astype(NPBF)
    ang = 2 * np.pi * np.outer(np.arange(128), k1) / 128.0
    c["F1f"] = np.concatenate([np.cos(ang), -np.sin(ang)], 1).astype(NPBF)
    n2 = np.arange(128)
    k2 = np.arange(128)
    kk = k1[None, :, None] + 128 * k2[None, None, :]
    ang = 2 * np.pi * (n2[:, None, None] * kk % NRING) / NRING
    c["Gr"] = np.cos(ang).astype(NPBF)
    c["Gi"] = (-np.sin(ang)).astype(NPBF)
    c["Gin"] = np.sin(ang).astype(NPBF)
    ang = 2 * np.pi * np.outer(k2, n2) / 128.0
    c["CS"] = np.concatenate([np.cos(ang), np.sin(ang)], 1).astype(NPBF)
    c["SC"] = np.concatenate([-np.sin(ang), np.cos(ang)], 1).astype(NPBF)
    wgt = np.where((k1 == 0) | (k1 == 64), 1.0, 2.0) / NRING
    n = 128 * n1map[None, None, :] + n2[None, :, None]
    ang = 2 * np.pi * ((n * k1[:, None, None]) % NRING) / NRING
    c["Hr"] = (wgt[:, None, None] * np.cos(ang)).astype(NPBF)
    c["Hi"] = (-wgt[:, None, None] * np.sin(ang)).astype(NPBF)
    pos = np.arange(NRING)
    valid = (pos < L) | (pos > NRING - L)
    tau = np.where(pos < L, pos, NRING - pos)
    tau = np.where(valid, tau, 0)
    t01 = np.linspace(0.0, 1.0, L, dtype=np.float32)[tau]
    w = (np.float32(2.0 * math.pi) * tau.astype(np.float32) / np.float32(L)).astype(np.float32)
    f = np.linspace(1e-4, 15, 16, dtype=np.float32)
    fw = (f[None, :] * w[:, None]).astype(np.float32)
    z = np.concatenate([t01[:, None], np.cos(fw), -np.sin(fw)], 1).astype(np.float32)
    c["zT"] = np.ascontiguousarray(z.T)
    t01m = np.where(valid, t01, np.float32(1e4)).astype(np.float32)
    c["T01"] = np.ascontiguousarray(np.broadcast_to(t01m[None, :], (128, NRING)))
    min_decay = math.log(1e-2) / 1.5
    max_decay = math.log(1e-2) / 0.3
    deltas = np.abs(np.linspace(min_decay, max_decay, 1024, dtype=np.float32))
    c["negd"] = np.ascontiguousarray((-deltas).reshape(8, 128).T.astype(np.float32))
    rows_seq = 128 if is_prompt else 64
    cols = np.arange(64)
    cstart = np.clip(cols - 8, 0, 48)
    colok = (cols[:, None] >= cstart[None, :]) & (cols[:, None] < cstart[None, :] + 16)

    def mask_tile(p, d):
        m = np.full((2, 64, 2, 64), MASKV * 8.0, np.float32)
        for a in range(2):
            r = 2 * p + a
            sbase = (r // rows_seq) * rows_seq
            rl = r - sbase
            r0 = min(max(rl - 4, 0), rows_seq - 8) + sbase
            for kr in range(2):
                krow = 2 * (p + d) + kr
                if r0 <= krow < r0 + 8:
                    m[kr, :, a, :] = np.where(colok, 0.0, MASKV * 8.0)
        return m.reshape(128, 128)
    c["mask_int"] = np.stack([mask_tile(10, d) for d in INTERIOR]).astype(NPBF)
    sp = []
    for p in SPEC_LIST:
        for d in SPECIAL[p]:
            sp.append(mask_tile(p, d))
    c["mask_sp"] = np.stack(sp).astype(NPBF)
    c["ident"] = np.eye(128).astype(NPBF)
    c["mflag"] = np.full((128, 1), 1.0 if is_prompt else 0.0, np.float32)
    return c


def _rpb_gather(rpb_l):
    out = np.zeros((7, 2, 64, 8, 2, 64), np.float32)
    kc = np.arange(64)[:, None]
    qc = np.arange(64)[None, :]
    co = kc - qc + 15
    cok = (co >= 0) & (co <= 30)
    coc = np.clip(co, 0, 30)
    for di, d in enumerate(range(-3, 4)):
        for kr in range(2):
            for a in range(2):
                ro = 2 * d + kr - a + 7
                if 0 <= ro <= 14:
                    for h in range(8):
                        out[di, kr, :, h, a, :] = np.where(cok, rpb_l[h, ro][coc], 0.0)
    return out.reshape(7, 128, 8, 128)


def build(debug=False):
    nc = bass.Bass("TRN2", target_bir_lowering=False)
    P = Prog(nc)
    D = {}

    def din(name, shape, dt=F32):
        D[name] = nc.dram_tensor(name, list(shape), dt, kind="ExternalInput").ap()
        return D[name]

    def dscr(name, shape, dt):
        kind = "ExternalOutput" if debug else "Internal"
        D[name] = nc.dram_tensor(name, list(shape), dt, kind=kind).ap()
        return D[name]

    din("x", [T, 1024])
    din("norm_g", [DEPTH, 128, 1024])
    din("w_in", [DEPTH, 1024, 4096])
    din("conv_w", [DEPTH, 128, 12, 3])
    din("conv_b", [DEPTH, 128, 12])
    din("f_w1", [DEPTH, 2, 33, 128])
    din("f_b1", [DEPTH, 128, 1])
    din("f_fr1", [DEPTH, 128, 1])
    din("f_w2", [DEPTH, 128, 128])
    din("f_b2", [DEPTH, 128, 1])
    din("f_fr2", [DEPTH, 128, 1])
    din("f_w3", [DEPTH, 128, 1024])
    din("hy_bias", [DEPTH, 128, 4])
    din("on_hy", [DEPTH, 128, 4])
    din("qn_g", [DEPTH, 128, 64])
    din("kn_g", [DEPTH, 128, 64])
    din("on_att", [DEPTH, 128, 512])
    din("rpbT", [DEPTH, 7, 128, 8, 128])
    din("w_out", [DEPTH, 1024, 1024])
    for nm, shp, dt in (("F1", [64, 130], BF16), ("F1f", [128, 130], BF16), ("Gr", [128, 65, 128], BF16),
                        ("Gi", [128, 65, 128], BF16), ("Gin", [128, 65, 128], BF16), ("CS", [128, 256], BF16),
                        ("SC", [128, 256], BF16), ("Hr", [65, 128, 64], BF16), ("Hi", [65, 128, 64], BF16),
                        ("zT", [33, NRING], F32), ("T01", [128, NRING], F32), ("negd", [128, 8], F32),
                        ("mask_int", [5, 128, 128], BF16), ("mask_sp", [38, 128, 128], BF16),
                        ("ident", [128, 128], BF16), ("mflag", [128, 1], F32)):
        din(nm, shp, dt)
    y_out = nc.dram_tensor("y", [T, 1024], F32, kind="ExternalOutput").ap()
    x1 = dscr("x1", [T, 1024], F32)
    uD = dscr("uD", [512, T], BF16)
    x0D = dscr("x0D", [512, T], BF16)
    gD = dscr("gD", [512, T], BF16)
    yD = dscr("yD", [512, T], F32)
    kfD = dscr("kfD", [512, NRING], BF16)
    KD = dscr("KD", [16, 128, 9 * 512], F32)
    QTD = dscr("QTD", [4, 128, T], BF16)
    KTD = dscr("KTD", [4, 128, T], BF16)
    VD = dscr("VD", [64, 128, 520], BF16)
    GD = dscr("GD", [64, 128, 512], BF16)
    ATD = dscr("ATD", [4, 128, T], BF16)

    bD = {k: Buf() for k in ("x1", "uD", "x0D", "gD", "yD", "kfD", "KD", "QTD", "KTD", "VD", "GD", "ATD", "y")}

    ident = P.sb("ident", [128, 128], BF16)
    b_ident = Buf()
    P.dma("sp", ident[:], D["ident"], w=[b_ident])
    rs_att = P.sb("rs_att", [128, 64], F32)
    b_rsatt = Buf()
    mflag = P.sb("mflag", [128, 1], F32)
    b_mflag = Buf()
    P.dma("sp", mflag[:], D["mflag"], w=[b_mflag])
    epsc = P.sb("epsc", [128, 1], F32)
    b_eps = Buf()
    P.op("pool", lambda e: e.memset(epsc[:], EPS), w=[b_eps])
    negpi = P.sb("negpi", [128, 1], F32)

    def run_pipelined(gens, width):
        gens = list(gens)
        active = []
        nxt = 0
        while nxt < len(gens) or active:
            if nxt < len(gens) and len(active) < width:
                active.append(gens[nxt])
                nxt += 1
            for g in list(active):
                try:
                    next(g)
                except StopIteration:
                    active.remove(g)

    def act(out, in_, func, r, w=(), pw=(), **kw):
        P.op("act", lambda e: e.activation(out=out, in_=in_, func=func, **kw), r=r, w=w, pw=pw)

    def rstd_from_ss(ss_ap, tmp_ap, out_ap, inv_n, bss, r=()):
        np_ = ss_ap.shape[0]
        P.op("act", lambda e: e.activation(out=tmp_ap, in_=ss_ap, func=ACTF.Sqrt, scale=inv_n, bias=epsc[0:np_, :]),
             r=[bss, b_eps] + list(r), w=[bss])
        P.op("dve", lambda e: e.reciprocal(out=out_ap, in_=tmp_ap), r=[bss], w=[bss])

    def stage_filter(l):
        P.mark()
        w1a = P.sb("w1a", [33, 128], F32)
        w1b = P.sb("w1b", [33, 128], F32)
        w2 = P.sb("w2", [128, 128], F32)
        w3f = P.sb("w3f", [128, 1024], F32)
        w3 = P.sb("w3", [128, 1024], BF16)
        sc = P.sb("fsc", [128, 12], F32)
        negd = P.sb("negd", [128, 8], F32)
        bw = Buf()
        P.dma("sp", w1a[:], D["f_w1"][l, 0], pw=[bw])
        P.dma("sp", w1b[:], D["f_w1"][l, 1], pw=[bw])
        P.dma("sp", w2[:], D["f_w2"][l], pw=[bw])
        P.dma("sp", w3f[:], D["f_w3"][l], pw=[bw])
        P.dma("sp", sc[:, 0:1], D["f_fr1"][l], pw=[bw])
        P.dma("sp", sc[:, 1:2], D["f_b1"][l], pw=[bw])
        P.dma("sp", sc[:, 3:4], D["f_fr2"][l], pw=[bw])
        P.dma("sp", sc[:, 4:5], D["f_b2"][l], pw=[bw])
        P.dma("sp", negd[:], D["negd"], pw=[bw])
        bw2 = Buf()
        P.op("pool", lambda e: e.tensor_copy(out=w3[:], in_=w3f[:]), r=[bw], pw=[bw2])
        P.op("pool", lambda e: e.tensor_tensor(out=sc[:, 2:3], in0=sc[:, 0:1], in1=sc[:, 1:2], op=ALU.mult), r=[bw], pw=[bw2])
        P.op("pool", lambda e: e.tensor_tensor(out=sc[:, 5:6], in0=sc[:, 3:4], in1=sc[:, 4:5], op=ALU.mult), r=[bw], pw=[bw2])
        bw3 = Buf()
        for (dst, src) in ((6, 0), (7, 2), (8, 3), (9, 5)):
            P.op("pool", lambda e, dst=dst, src=src: e.tensor_scalar(out=sc[:, dst:dst + 1], in0=sc[:, src:src + 1], scalar1=1.0 / 3.0, scalar2=None,
                                                                     op0=ALU.mult), r=[bw, bw2], pw=[bw3])
        bw2 = bw3 if False else bw2
        W = 4
        zt = [P.sb("zt", [33, 2, 512], F32) for _ in range(W)]
        bz = [Buf() for _ in range(W)]
        tt = [P.sb("t01t", [128, 2, 512], F32) for _ in range(W)]
        btt = [Buf() for _ in range(W)]
        ps1 = [P.ps("fps1", [128, 512]) for _ in range(W)]
        bps1 = [Buf() for _ in range(W)]
        ps3 = [P.ps("fps3", [128, 512]) for _ in range(W)]
        bps3 = [Buf() for _ in range(W)]
        tA = [P.sb("tA", [128, 512], F32) for _ in range(W)]; btA = [Buf() for _ in range(W)]
        tB = [P.sb("tB", [128, 512], F32) for _ in range(W)]; btB = [Buf() for _ in range(W)]
        tC = [P.sb("tC", [128, 512], F32) for _ in range(W)]; btC = [Buf() for _ in range(W)]
        h1 = [P.sb("h1", [128, 512], F32) for _ in range(W)]; bh1 = [Buf() for _ in range(W)]
        h2 = [P.sb("h2", [128, 512], BF16) for _ in range(W)]; bh2 = [Buf() for _ in range(W)]
        dec = [P.sb("dec", [128, 512], F32) for _ in range(W)]
        bdec = [Buf() for _ in range(W)]
        kf = [P.sb("kf", [128, 512], BF16) for _ in range(W)]
        bkf = [Buf() for _ in range(W)]

        def sin_layer(s, frc, fbc, out_ap, bout):
            act(tA[s][:], ps1[s][:], ACTF.Sin, r=[bps1[s], bw, bw2, bw3], w=[btA[s]], scale=sc[:, frc:frc + 1], bias=sc[:, fbc:fbc + 1])
            yield
            P.op("dve", lambda e: e.tensor_tensor(out=tB[s][:], in0=tA[s][:], in1=tA[s][:], op=ALU.mult), r=[btA[s]], w=[btB[s]])
            yield
            P.op("dve", lambda e: e.tensor_scalar(out=tB[s][:], in0=tB[s][:], scalar1=-4.0, scalar2=3.0,
                                                  op0=ALU.mult, op1=ALU.add), r=[btB[s]], w=[btB[s]])
            yield
            P.op("pool", lambda e: e.tensor_tensor(out=out_ap, in0=tB[s][:], in1=tA[s][:], op=ALU.mult), r=[btA[s], btB[s]], w=[bout])
            yield

        def block_gen(bp):
            s = bp % W
            for i, b in enumerate((bp, bp + 16)):
                P.dma("sp", zt[s][:, i, :], D["zT"][:, b * 512:(b + 1) * 512], **({"w": [bz[s]]} if i == 0 else {"pw": [bz[s]]}))
                P.dma("sp", tt[s][:, i, :], D["T01"][:, b * 512:(b + 1) * 512], **({"w": [btt[s]]} if i == 0 else {"pw": [btt[s]]}))
            P.op("pe", lambda e: e.matmul(ps1[s][:], lhsT=w1a[:], rhs=zt[s][:, 0, :], start=True, stop=False), r=[bw, bz[s]], w=[bps1[s]])
            P.op("pe", lambda e: e.matmul(ps1[s][:], lhsT=w1b[:], rhs=zt[s][:, 1, :], start=False, stop=True), r=[bw, bz[s]], pw=[bps1[s]])
            yield
            yield from sin_layer(s, 6, 7, h1[s][:], bh1[s])
            P.op("pe", lambda e: e.matmul(ps1[s][:], lhsT=w2[:], rhs=h1[s][:], start=True, stop=True), r=[bw, bh1[s]], w=[bps1[s]])
            yield
            yield from sin_layer(s, 8, 9, h2[s][:], bh2[s])
            for i, b in enumerate((bp, bp + 16)):
                for cc in range(4):
                    q = cc
                    col0 = i * 512 + cc * 128
                    ndc = i * 4 + cc
                    P.op("pe", lambda e, q=q, col0=col0: e.matmul(ps3[q][:], lhsT=w3[:, col0:col0 + 128], rhs=h2[s][:], start=True, stop=True),
                         r=[bw2, bh2[s]], w=[bps3[q]])
                    act(dec[q][:], tt[s][:, i, :], ACTF.Exp, r=[btt[s], bw], w=[bdec[q]], scale=negd[:, ndc:ndc + 1])
                    yield
                    P.op("dve", lambda e, q=q: e.tensor_tensor(out=kf[q][:], in0=ps3[q][:], in1=dec[q][:], op=ALU.mult),
                         r=[bps3[q], bdec[q]], w=[bkf[q]])
                    P.dma(SQ, kfD[cc * 128:(cc + 1) * 128, b * 512:(b + 1) * 512], kf[q][:], r=[bkf[q]], pw=[bD["kfD"]])
                    yield

        run_pipelined([block_gen(bp) for bp in range(16)], W)
        P.barrier()
        P.release()

    def fft_consts():
        t = {}
        bc = Buf()
        for nm, shp in (("F1", [64, 130]), ("F1f", [128, 130]), ("Gr", [128, 65, 128]), ("Gi", [128, 65, 128]),
                        ("Gin", [128, 65, 128]), ("CS", [128, 256]), ("SC", [128, 256]), ("Hr", [65, 128, 64]),
                        ("Hi", [65, 128, 64])):
            t[nm] = P.sb("c" + nm, shp, BF16)
            P.dma("sp", t[nm][:], D[nm], pw=[bc])
        return t, bc

    def fft_bufs(kpart):
        fb = {}
        fb["Vc"] = [P.sb("Vc", [kpart, 32, 128], BF16) for _ in range(2)]
        fb["bV"] = [Buf(), Buf()]
        fb["psA"] = [P.ps("psA", [128, 3, 130]) for _ in range(2)]
        fb["bpsA"] = [Buf(), Buf()]
        fb["psU"] = [P.ps("psU", [128, 8, 2, 32]) for _ in range(2)]
        fb["bpsU"] = [Buf(), Buf()]
        fb["A_sb"] = P.sb("A_sb", [128, 32, 130], BF16)
        fb["bA"] = Buf()
        return fb

    def fft_forward(fb, bc, src, bsrc, g, F1t):
        vq = g % 2
        Vc, bV = fb["Vc"][vq], fb["bV"][vq]
        psA, bpsA, A_sb, bA = fb["psA"], fb["bpsA"], fb["A_sb"], fb["bA"]
        P.dma("sp", Vc[:], src[g * 32:(g + 1) * 32, :].rearrange("c (j n) -> j c n", n=128), r=[bsrc], w=[bV])
        for c0 in range(0, 32, 3):
            nn = min(3, 32 - c0)
            q = (c0 // 3) % 2
            for i in range(nn):
                P.op("pe", lambda e, q=q, i=i, c0=c0: e.matmul(psA[q][:, i, :], lhsT=Vc[:, c0 + i, :], rhs=F1t[:], start=True, stop=True),
                     r=[bV, bc], **({"w": [bpsA[q]]} if i == 0 else {"pw": [bpsA[q]]}))
            act(A_sb[:, c0:c0 + nn, :], psA[q][:, 0:nn, :], ACTF.Copy, r=[bpsA[q]], pw=[bA])

    def fft_stage2(C, bc, fb, evac):
        psU, bpsU, A_sb, bA = fb["psU"], fb["bpsU"], fb["A_sb"], fb["bA"]
        for bk in range(9):
            q = bk % 2
            nk = min(8, 65 - bk * 8)
            first = True
            for kk in range(nk):
                k1 = bk * 8 + kk
                Ar = A_sb[:, :, k1]
                Ai = A_sb[:, :, 65 + k1]
                for (o, lt, rh, st) in ((0, C["Gr"], Ar, True), (0, C["Gin"], Ai, False), (1, C["Gi"], Ar, True), (1, C["Gr"], Ai, False)):
                    P.op("pe", lambda e, q=q, kk=kk, o=o, lt=lt, rh=rh, st=st, k1=k1: e.matmul(
                        psU[q][:, kk, o, :], lhsT=lt[:, k1, :], rhs=rh, start=st, stop=not st),
                        r=[bA, bc], **({"w": [bpsU[q]]} if first else {"pw": [bpsU[q]]}))
                    first = False
            evac(bk, nk, psU[q], bpsU[q])

    def stage_filter_fft(l, C, bc):
        P.mark()
        fb = fft_bufs(128)
        Ko = [P.sb("Ko", [128, 512], F32) for _ in range(2)]
        bKo = [Buf(), Buf()]
        for g in range(16):
            fft_forward(fb, bc, kfD, bD["kfD"], g, C["F1f"])

            def evac(bk, nk, ps, bps, g=g):
                q = bk % 2
                act(Ko[q][:, 0:nk * 64], ps[:, 0:nk, :, :].rearrange("p a b c -> p (a b c)"), ACTF.Copy, r=[bps], w=[bKo[q]])
                P.dma(SQ, KD[g, :, bk * 512:bk * 512 + nk * 64], Ko[q][:, 0:nk * 64], r=[bKo[q]], pw=[bD["KD"]])
            fft_stage2(C, bc, fb, evac)
        P.barrier()
        P.release()

    def stage_conv(l, C, bc):
        P.mark()
        fb = fft_bufs(64)
        Ks = [P.sb("Ks", [128, 9 * 512], F32) for _ in range(2)]; bKs = [Buf(), Buf()]
        Us = [P.sb("Us", [128, 512], F32) for _ in range(2)]; bUs = [Buf(), Buf()]
        t1 = [P.sb("t1", [128, 256], F32) for _ in range(2)]; bt1 = [Buf(), Buf()]
        t2 = [P.sb("t2", [128, 256], F32) for _ in range(2)]; bt2 = [Buf(), Buf()]
        Yrr = [P.sb("Yr", [128, 32, 65], BF16) for _ in range(2)]
        Yir = [P.sb("Yi", [128, 32, 65], BF16) for _ in range(2)]
        bYr = [Buf(), Buf()]
        Dsb = P.sb("Dsb", [65, 32, 2, 128], BF16); bDs = Buf()
        Yo = P.sb("Yo", [64, 32, 128], F32); bYo = Buf()
        psD = [P.ps("psD", [65, 2, 256]) for _ in range(2)]
        bpsD = [Buf(), Buf()]
        psY = [P.ps("psY", [64, 16, 32]) for _ in range(2)]
        bpsY = [Buf(), Buf()]
        def fwd1(g):
            kq = g % 2
            P.dma("sp", Ks[kq][:, 0:4160], KD[g, :, 0:4160], r=[bD["KD"]], w=[bKs[kq]])
            fft_forward(fb, bc, uD, bD["uD"], g, C["F1"])

        def fwd2(g):
            kq = g % 2
            Yr, Yi, bY = Yrr[g % 2], Yir[g % 2], bYr[g % 2]

            def evac(bk, nk, ps, bps, kq=kq):
                q = bk % 2
                eng = "pool" if bk in (1, 3, 5, 7) else "dve"
                n = nk * 64
                if eng == "pool":
                    act(Us[q][:, 0:n], ps[:, 0:nk, :, :].rearrange("p a b c -> p (a b c)"), ACTF.Copy, r=[bps], w=[bUs[q]])
                    U4 = Us[q][:, 0:n].rearrange("p (a b c) -> p a b c", b=2, c=32)
                    bu_ = bUs[q]
                else:
                    U4 = ps[:, 0:nk, :, :]
                    bu_ = bps
                K4 = Ks[kq][:, bk * 512:bk * 512 + n].rearrange("p (a b c) -> p a b c", b=2, c=32)
                T1 = t1[q][:, 0:nk * 32].rearrange("p (a c) -> p a c", c=32)
                T2 = t2[q][:, 0:nk * 32].rearrange("p (a c) -> p a c", c=32)
                yr = Yr[:, :, bk * 8:bk * 8 + nk].rearrange("p c a -> p a c")
                yi = Yi[:, :, bk * 8:bk * 8 + nk].rearrange("p c a -> p a c")
                bk_ = bKs[kq]
                P.op(eng, lambda e: e.tensor_tensor(out=T1, in0=U4[:, :, 0, :], in1=K4[:, :, 0, :], op=ALU.mult), r=[bu_, bk_], w=[bt1[q]])
                P.op(eng, lambda e: e.tensor_tensor(out=T2, in0=U4[:, :, 1, :], in1=K4[:, :, 1, :], op=ALU.mult), r=[bu_, bk_], w=[bt2[q]])
                P.op(eng, lambda e: e.tensor_tensor(out=yr, in0=T1, in1=T2, op=ALU.subtract), r=[bt1[q], bt2[q]], pw=[bY])
                P.op(eng, lambda e: e.tensor_tensor(out=T1, in0=U4[:, :, 0, :], in1=K4[:, :, 1, :], op=ALU.mult), r=[bu_, bk_], w=[bt1[q]])
                P.op(eng, lambda e: e.tensor_tensor(out=T2, in0=U4[:, :, 1, :], in1=K4[:, :, 0, :], op=ALU.mult), r=[bu_, bk_], w=[bt2[q]])
                P.op(eng, lambda e: e.tensor_tensor(out=yi, in0=T1, in1=T2, op=ALU.add), r=[bt1[q], bt2[q]], pw=[bY])
            fft_stage2(C, bc, fb, evac)

        def invA(g):
            Yr, Yi, bY = Yrr[g % 2], Yir[g % 2], bYr[g % 2]
            for c0 in range(0, 32, 2):
                q = (c0 // 2) % 2
                for i in range(2):
                    c = c0 + i
                    P.op("pe", lambda e, q=q, i=i, c=c: e.matmul(psD[q][:, i, :], lhsT=Yr[:, c, :], rhs=C["CS"][:], start=True, stop=False),
                         r=[bY, bc], **({"w": [bpsD[q]]} if i == 0 else {"pw": [bpsD[q]]}))
                    P.op("pe", lambda e, q=q, i=i, c=c: e.matmul(psD[q][:, i, :], lhsT=Yi[:, c, :], rhs=C["SC"][:], start=False, stop=True),
                         r=[bY, bc], pw=[bpsD[q]])
                act(Dsb[:, c0:c0 + 2, :, :].rearrange("p c b n -> p c (b n)"), psD[q][:], ACTF.Copy, r=[bpsD[q]], pw=[bDs])

        def final(g):
            for nb in range(8):
                q = nb % 2
                for i in range(16):
                    n2 = nb * 16 + i
                    P.op("pe", lambda e, q=q, i=i, n2=n2: e.matmul(psY[q][:, i, :], lhsT=C["Hr"][:, n2, :], rhs=Dsb[:, :, 0, n2], start=True, stop=False),
                         r=[bDs, bc], **({"w": [bpsY[q]]} if i == 0 else {"pw": [bpsY[q]]}))
                    P.op("pe", lambda e, q=q, i=i, n2=n2: e.matmul(psY[q][:, i, :], lhsT=C["Hi"][:, n2, :], rhs=Dsb[:, :, 1, n2], start=False, stop=True),
                         r=[bDs, bc], pw=[bpsY[q]])
                act(Yo[:, :, nb * 16:(nb + 1) * 16].rearrange("p c n -> p n c"), psY[q][:], ACTF.Copy,
                    r=[bpsY[q]], **({"w": [bYo]} if nb == 0 else {"pw": [bYo]}))
            P.dma(SQ, yD[g * 32:(g + 1) * 32, :].rearrange("c (j n) -> j c n", n=128), Yo[:], r=[bYo], pw=[bD["yD"]])

        fwd1(0)
        fwd2(0)
        for g in range(16):
            if g + 1 < 16:
                fwd1(g + 1)
            invA(g)
            if g + 1 < 16:
                fwd2(g + 1)
            final(g)
        P.barrier()
        P.release()

    def stage_proj(l, xsrc, bxsrc):
        P.mark()
        gt = P.sb("gt", [128, 1024], F32); bg = Buf()
        P.dma("sp", gt[:], D["norm_g"][l], w=[bg])
        cw = P.sb("cw", [128, 12, 3], F32)
        cb = P.sb("cb", [128, 12], F32)
        qg = P.sb("qg", [128, 64], F32)
        kg = P.sb("kg", [128, 64], F32)
        bsm = Buf()
        P.dma("sp", cw[:], D["conv_w"][l], pw=[bsm])
        P.dma("sp", cb[:], D["conv_b"][l], pw=[bsm])
        P.dma("sp", qg[:], D["qn_g"][l], pw=[bsm])
        P.dma("sp", kg[:], D["kn_g"][l], pw=[bsm])
        wA = P.sb("wA", [128, 8, 2048], BF16); bwA = Buf()
        wst = [P.sb("wst", [128, 8, 128], F32) for _ in range(2)]; bwst = [Buf(), Buf()]
        wv = D["w_in"][l].rearrange("(k p) c -> p k c", p=128)
        for j in range(16):
            q = j % 2
            P.dma("sp", wst[q][:], wv[:, :, 2048 + j * 128:2048 + (j + 1) * 128], w=[bwst[q]])
            P.op("act", lambda e, q=q, j=j: e.activation(out=wA[:, :, j * 128:(j + 1) * 128], in_=wst[q][:], func=ACTF.Copy), r=[bwst[q]], pw=[bwA])
        hT = P.sb("hT", [128, 8, SEG + 2], BF16); bhT = Buf()
        xt = [P.sb("xt", [128, 1024], F32) for _ in range(2)]; bx = [Buf(), Buf()]
        ss = [P.sb("ss", [128, 4], F32) for _ in range(2)]; bss = [Buf(), Buf()]
        hb = [P.sb("hb", [128, 1024], BF16) for _ in range(2)]; bh = [Buf(), Buf()]
        ptr = P.ps("ptr", [128, 8, 128], BF16); bptr = Buf()
        edge = P.sb("edge", [128, 8, 128], BF16); bedge = Buf()
        state = {"i": 0}

        def norm_tile(row0, dst_ap, bdst, dst_w):
            q = state["i"] % 2
            state["i"] += 1
            P.dma("sp", xt[q][:], xsrc[row0:row0 + 128, :], r=[bxsrc], w=[bx[q]])
            P.op("act", lambda e: e.activation(out=hb[q][:], in_=xt[q][:], func=ACTF.Square, accum_out=ss[q][:, 0:1]), r=[bx[q]], w=[bh[q], bss[q]])
            yield
            P.op("act", lambda e: e.activation(out=ss[q][:, 1:2], in_=ss[q][:, 0:1], func=ACTF.Sqrt, scale=1.0 / 1024, bias=epsc[:]),
                 r=[bss[q], b_eps], w=[bss[q]])
            yield
            P.op("dve", lambda e: e.reciprocal(out=ss[q][:, 2:3], in_=ss[q][:, 1:2]), r=[bss[q]], w=[bss[q]])
            yield
            P.op("dve", lambda e: e.scalar_tensor_tensor(out=hb[q][:], in0=xt[q][:], scalar=ss[q][:, 2:3], in1=gt[:], op0=ALU.mult, op1=ALU.mult),
                 r=[bx[q], bss[q], bg], w=[bh[q]])
            yield
            for k in range(8):
                P.op("pe", lambda e, k=k: e.transpose(out=ptr[:, k, :], in_=hb[q][:, k * 128:(k + 1) * 128], identity=ident[:]),
                     r=[bh[q], b_ident], **({"w": [bptr]} if k == 0 else {"pw": [bptr]}))
            yield
            P.op("act", lambda e: e.activation(out=dst_ap, in_=ptr[:], func=ACTF.Copy), r=[bptr], **{dst_w: [bdst]})
            yield

        wh = [P.sb("wh", [128, 8, 128], BF16) for _ in range(2)]; bwh = [Buf(), Buf()]
        psz = [P.ps("psz", [128, 512]) for _ in range(2)]; bpsz = [Buf(), Buf()]
        R = [P.sb("R", [128, SEG], F32) for _ in range(2)]
        bRk = [[Buf() for _ in range(9)] for _ in range(2)]
        Rb = [P.sb("Rb", [128, SEG], BF16) for _ in range(2)]; bRb = [Buf(), Buf()]
        psq = [P.ps("psq", [128, 512]) for _ in range(4)]; bpsq = [Buf() for _ in range(4)]
        sq = P.sb("sq", [128, 512], F32); bsq = Buf()
        ssq = P.sb("ssq", [128, 24], F32); bssq = Buf()
        qn = P.sb("qn", [128, 512], F32); bqn = Buf()
        qnb2 = [[P.sb("qnb", [128, 512], BF16) for _ in range(2)] for _ in range(2)]
        bqnb2 = [[Buf(), Buf()], [Buf(), Buf()]]
        ptq = P.ps("ptq", [128, 4, 128], BF16); bptq = Buf()
        QTo = [P.sb("QTo", [128, 4, 256], BF16) for _ in range(2)]; bQTo = [Buf(), Buf()]
        KTo = [P.sb("KTo", [128, 4, 256], BF16) for _ in range(2)]; bKTo = [Buf(), Buf()]
        Vo = [P.sb("Vo", [128, 2, 8, 65], BF16) for _ in range(2)]; bVo = [Buf(), Buf()]
        Go = [P.sb("Go", [128, 2, 512], BF16) for _ in range(2)]; bGo = [Buf(), Buf()]
        for q in range(2):
            P.op("pool", lambda e, q=q: e.memset(Vo[q][:], 1.0), w=[bVo[q]])

        blocks = [(o, min(510, SEG - o)) for o in range(0, SEG, 510)]
        wcnt = {"i": 0}
        for s in range(2):
            base = s * SEG
            run_pipelined([norm_tile(base + t * 128, hT[:, :, 1 + t * 128:1 + (t + 1) * 128], bhT, "w" if t == 0 else "pw")
                           for t in range(32)], 2)
            if s == 0:
                P.op("pool", lambda e: e.memset(hT[:, :, 0:1], 0.0), pw=[bhT])
                run_pipelined([norm_tile(SEG, edge[:], bedge, "w")], 1)
                P.op("dve", lambda e: e.tensor_scalar(out=hT[:, :, SEG + 1:SEG + 2], in0=edge[:, :, 0:1], scalar1=mflag[:, 0:1], scalar2=None,
                                                      op0=ALU.mult), r=[bedge, b_mflag], pw=[bhT])
            else:
                P.op("pool", lambda e: e.memset(hT[:, :, SEG + 1:SEG + 2], 0.0), pw=[bhT])
                run_pipelined([norm_tile(SEG - 128, edge[:], bedge, "w")], 1)
                P.op("dve", lambda e: e.tensor_scalar(out=hT[:, :, 0:1], in0=edge[:, :, 127:128], scalar1=mflag[:, 0:1], scalar2=None,
                                                      op0=ALU.mult), r=[bedge, b_mflag], pw=[bhT])
            for j in (range(4) if 'hy' in build.parts else ()):
                for kind, ch in (("x1", 4 + j), ("vv", 8 + j), ("x0", j), ("g", 12 + j)):
                    wq = wcnt["i"] % 2
                    wcnt["i"] += 1
                    P.dma("sp", wst[wq][:], wv[:, :, ch * 128:(ch + 1) * 128], w=[bwst[wq]])
                    P.op("act", lambda e, wq=wq: e.activation(out=wh[wq][:], in_=wst[wq][:], func=ACTF.Copy), r=[bwst[wq]], w=[bwh[wq]])
                    rq = 0 if kind == "x1" else 1
                    for bi, (o0, wd) in enumerate(blocks):
                        pq = bi % 2
                        for k in range(8):
                            P.op("pe", lambda e, pq=pq, k=k, wq=wq, o0=o0, wd=wd: e.matmul(
                                psz[pq][:, 0:wd + 2], lhsT=wh[wq][:, k, :], rhs=hT[:, k, o0:o0 + wd + 2], start=(k == 0), stop=(k == 7)),
                                r=[bwh[wq], bhT], **({"w": [bpsz[pq]]} if k == 0 else {"pw": [bpsz[pq]]}))
                        bk_ = bRk[rq][bi]
                        fw = {"w": [bRb[0]]} if bi == 0 else {"pw": [bRb[0]]}
                        if kind == "g":
                            act(Rb[1][:, o0:o0 + wd], psz[pq][:, 1:wd + 1], ACTF.Silu, r=[bpsz[pq]], **({"w": [bRb[1]]} if bi == 0 else {"pw": [bRb[1]]}))
                        else:
                            act(R[rq][:, o0:o0 + wd], psz[pq][:, 1:wd + 1], ACTF.Identity, r=[bpsz[pq], bsm],
                                scale=cw[:, ch, 1:2], bias=cb[:, ch:ch + 1], w=[bk_])
                            P.op("dve", lambda e, pq=pq, o0=o0, wd=wd, rq=rq, ch=ch: e.scalar_tensor_tensor(
                                out=R[rq][:, o0:o0 + wd], in0=psz[pq][:, 0:wd], scalar=cw[:, ch, 0:1], in1=R[rq][:, o0:o0 + wd],
                                op0=ALU.mult, op1=ALU.add), r=[bpsz[pq], bsm, bk_], pw=[bk_])
                            if kind == "x0":
                                P.op("dve", lambda e, pq=pq, o0=o0, wd=wd, rq=rq, ch=ch: e.scalar_tensor_tensor(
                                    out=Rb[0][:, o0:o0 + wd], in0=psz[pq][:, 2:wd + 2], scalar=cw[:, ch, 2:3], in1=R[rq][:, o0:o0 + wd],
                                    op0=ALU.mult, op1=ALU.add), r=[bpsz[pq], bsm, bk_], **fw)
                            else:
                                P.op("dve", lambda e, pq=pq, o0=o0, wd=wd, rq=rq, ch=ch: e.scalar_tensor_tensor(
                                    out=R[rq][:, o0:o0 + wd], in0=psz[pq][:, 2:wd + 2], scalar=cw[:, ch, 2:3], in1=R[rq][:, o0:o0 + wd],
                                    op0=ALU.mult, op1=ALU.add), r=[bpsz[pq], bsm, bk_], pw=[bk_])
                            if kind == "vv":
                                P.op("pool", lambda e, o0=o0, wd=wd: e.tensor_tensor(out=Rb[0][:, o0:o0 + wd], in0=R[0][:, o0:o0 + wd],
                                                                                    in1=R[1][:, o0:o0 + wd], op=ALU.mult),
                                     r=[bRk[0][bi], bRk[1][bi]], **fw)
                    rows = slice(j * 128, (j + 1) * 128)
                    colsl = slice(base, base + SEG)
                    if kind == "vv":
                        P.dma(SQ, uD[rows, colsl], Rb[0][:], r=[bRb[0]], pw=[bD["uD"]])
                    elif kind == "x0":
                        P.dma(SQ, x0D[rows, colsl], Rb[0][:], r=[bRb[0]], pw=[bD["x0D"]])
                    elif kind == "g":
                        P.dma(SQ, gD[rows, colsl], Rb[1][:], r=[bRb[1]], pw=[bD["gD"]])
            def partA(t):
                tok = slice(1 + t * 128, 1 + (t + 1) * 128)
                tg = s * 32 + t
                ob = (tg // 2) % 2
                oi = t % 2
                par = t % 2
                for m in (2, 3, 0, 1):
                    for k in range(8):
                        P.op("pe", lambda e, m=m, k=k, tok=tok: e.matmul(psq[m][:], lhsT=hT[:, k, tok], rhs=wA[:, k, m * 512:(m + 1) * 512],
                                                                        start=(k == 0), stop=(k == 7)),
                             r=[bhT, bwA], **({"w": [bpsq[m]]} if k == 0 else {"pw": [bpsq[m]]}))
                    if m == 2:
                        act(Vo[ob][:, oi, :, 0:64], psq[2][:].rearrange("p (h d) -> p h d", d=64), ACTF.Copy, r=[bpsq[2]], pw=[bVo[ob]])
                    elif m == 3:
                        act(Go[ob][:, oi, :], psq[3][:], ACTF.Silu, r=[bpsq[3]], **({"w": [bGo[ob]]} if oi == 0 else {"pw": [bGo[ob]]}))
                for m, gsb in ((0, qg), (1, kg)):
                    qb, bqb = qnb2[par][m], bqnb2[par][m]
                    act(sq[:], psq[m][:], ACTF.Square, r=[bpsq[m]], w=[bsq])
                    P.op("dve", lambda e: e.tensor_reduce(out=ssq[:, 0:8], in_=sq[:].rearrange("p (h d) -> p h d", d=64), axis=AX.X, op=ALU.add),
                         r=[bsq], w=[bssq])
                    rstd_from_ss(ssq[:, 0:8], ssq[:, 8:16], ssq[:, 16:24], 1.0 / 64, bssq)
                    P.op("dve", lambda e, m=m: e.tensor_tensor(out=qn[:].rearrange("p (h d) -> p h d", d=64),
                                                               in0=psq[m][:].rearrange("p (h d) -> p h d", d=64),
                                                               in1=ssq[:, 16:24].unsqueeze(2).to_broadcast([128, 8, 64]), op=ALU.mult),
                         r=[bpsq[m], bssq], w=[bqn])
                    P.op("pool", lambda e, gsb=gsb, qb=qb: e.tensor_tensor(out=qb[:].rearrange("p (h d) -> p h d", d=64),
                                                                           in0=qn[:].rearrange("p (h d) -> p h d", d=64),
                                                                           in1=gsb[:].unsqueeze(1).to_broadcast([128, 8, 64]), op=ALU.mult),
                         r=[bqn, bsm], w=[bqb])

            def partB(t):
                tg = s * 32 + t
                ob = (tg // 2) % 2
                oi = t % 2
                par = t % 2
                for m, dst, bdst in ((0, QTo, bQTo), (1, KTo, bKTo)):
                    qb, bqb = qnb2[par][m], bqnb2[par][m]
                    for hp in range(4):
                        P.op("pe", lambda e, hp=hp, qb=qb: e.transpose(out=ptq[:, hp, :], in_=qb[:, hp * 128:(hp + 1) * 128], identity=ident[:]),
                             r=[bqb, b_ident], **({"w": [bptq]} if hp == 0 else {"pw": [bptq]}))
                    P.op("dve", lambda e, dst=dst, ob=ob, oi=oi: e.tensor_copy(out=dst[ob][:, :, oi * 128:(oi + 1) * 128], in_=ptq[:]),
                         r=[bptq], **({"w": [bdst[ob]]} if oi == 0 else {"pw": [bdst[ob]]}))
                if oi == 1:
                    t0 = (tg - 1) * 128
                    P.dma(SQ, QTD[:, :, t0:t0 + 256].rearrange("a p t -> p a t"), QTo[ob][:], r=[bQTo[ob]], pw=[bD["QTD"]])
                    P.dma(SQ, KTD[:, :, t0:t0 + 256].rearrange("a p t -> p a t"), KTo[ob][:], r=[bKTo[ob]], pw=[bD["KTD"]])
                    P.dma(SQ, VD[tg - 1:tg + 1].rearrange("c p f -> p c f"), Vo[ob][:].rearrange("p c h f -> p c (h f)"), r=[bVo[ob]], pw=[bD["VD"]])
                    P.dma(SQ, GD[tg - 1:tg + 1].rearrange("c p f -> p c f"), Go[ob][:], r=[bGo[ob]], pw=[bD["GD"]])

            if 'att' in build.parts:
                for t in range(33):
                    if t < 32:
                        partA(t)
                    if t >= 1:
                        partB(t - 1)
        P.barrier()
        P.release()

    def stage_attn(l):
        P.mark()
        KT = P.sb("KT", [128, 4, T], BF16); bKT = [Buf() for _ in range(8)]
        V = P.sb("V", [128, 64, 520], BF16); bV = [Buf() for _ in range(8)]
        ona = P.sb("ona", [128, 512], F32); bona = Buf()
        P.dma("sp", ona[:], D["on_att"][l], w=[bona])
        rp = P.sb("rp", [128, 7, 8, 128], F32); brp = Buf()
        for d in range(7):
            P.dma("sp", rp[:, d], D["rpbT"][l, d], pw=[brp])
        mint = P.sb("mint", [128, 5, 128], BF16); bmint = Buf()
        P.dma("sp", mint[:], D["mask_int"].rearrange("d p q -> p d q"), w=[bmint])
        bint = P.sb("bint", [128, 5, 8, 128], BF16); bbint = Buf()
        kv_state = {"n": 0}

        def ensure_kv(upto):
            while kv_state["n"] <= min(7, upto):
                tq = kv_state["n"]
                P.dma("sp", KT[:, :, tq * 1024:(tq + 1) * 1024], KTD[:, :, tq * 1024:(tq + 1) * 1024].rearrange("a p t -> p a t"),
                      r=[bD["KTD"]], w=[bKT[tq]])
                P.dma("sp", V[:, tq * 8:(tq + 1) * 8, :], VD[tq * 8:(tq + 1) * 8].rearrange("c p f -> p c f"), r=[bD["VD"]], w=[bV[tq]])
                kv_state["n"] += 1
        ensure_kv(0)
        for di, d in enumerate(INTERIOR):
            P.op("dve", lambda e, di=di, d=d: e.scalar_tensor_tensor(out=bint[:, di], in0=rp[:, d + 3], scalar=8.0,
                                                                      in1=mint[:, di, :].unsqueeze(1).to_broadcast([128, 8, 128]),
                                                                      op0=ALU.mult, op1=ALU.add), r=[brp, bmint], pw=[bbint])
        msp = P.sb("msp", [128, 6, 128], BF16); bmsp = Buf()
        bsp = P.sb("bsp", [128, 6, 8, 128], BF16); bbsp = Buf()
        QT = [P.sb("QT", [128, 4, 2, 128], BF16) for _ in range(2)]; bQT = [Buf(), Buf()]
        for q in range(2):
            P.op("pool", lambda e, q=q: e.memset(QT[q][:], 0.0), w=[bQT[q]])
        Gt = [P.sb("Gt", [128, 512], BF16) for _ in range(2)]; bGt = [Buf(), Buf()]
        NR = 5
        psS = [P.ps("psS", [128, 2, 2, 128]) for _ in range(NR)]; bpsS = [Buf() for _ in range(NR)]
        NRP = 6
        Pt = [P.sb("Pt", [128, 2, 2, 128], BF16) for _ in range(NRP)]; bPt = [Buf() for _ in range(NRP)]
        pring = {"i": 0}
        ring = {"i": 0}
        psO = [P.ps("psO", [128, 4, 65]) for _ in range(2)]; bpsO = [Buf(), Buf()]
        den = P.sb("den", [128, 16], F32); bden = Buf()
        yat = P.sb("yat", [128, 512], F32); byat = Buf()
        junk = P.sb("junk2", [128, 512], BF16); bjunk = Buf()
        ssa = P.sb("ssa", [128, 4], F32); bssa = Buf()
        ya2 = P.sb("ya2", [128, 512], F32); bya2 = Buf()
        yab = P.sb("yab", [128, 512], BF16); byab = Buf()
        pta = P.ps("pta", [128, 4, 128], BF16); bpta = Buf()
        ATo = [P.sb("ATo", [128, 4, 256], BF16) for _ in range(2)]; bATo = [Buf(), Buf()]
        sp_state = {"off": 0}

        def setup(p):
            q = p % 2
            if p in SPECIAL:
                dl = SPECIAL[p]
                nd = len(dl)
                so_ = sp_state["off"]
                P.dma("sp", msp[:, 0:nd, :], D["mask_sp"][so_:so_ + nd].rearrange("d p q -> p d q"), w=[bmsp])
                sp_state["off"] += nd
                for di, d in enumerate(dl):
                    P.op("dve", lambda e, di=di, d=d: e.scalar_tensor_tensor(out=bsp[:, di], in0=rp[:, d + 3], scalar=8.0,
                                                                              in1=msp[:, di, :].unsqueeze(1).to_broadcast([128, 8, 128]),
                                                                              op0=ALU.mult, op1=ALU.add),
                         r=[brp, bmsp], **({"w": [bbsp]} if di == 0 else {"pw": [bbsp]}))
                btile, bbt = bsp, bbsp
            else:
                dl = INTERIOR
                nd = 5
                btile, bbt = bint, bbint
            qsrc = QTD[:, :, p * 128:(p + 1) * 128].rearrange("a p t -> p a t")
            P.dma("sp", QT[q][0:64, :, 0, :], qsrc[0:64], r=[bD["QTD"]], w=[bQT[q]])
            P.dma("sp", QT[q][64:128, :, 1, :], qsrc[64:128], r=[bD["QTD"]], pw=[bQT[q]])
            P.dma("sp", Gt[q][:], GD[p], r=[bD["GD"]], w=[bGt[q]])
            return dict(p=p, q=q, dl=dl, nd=nd, btile=btile, bbt=bbt)

        def partA(cx, hp):
            p, q, dl, nd, btile, bbt = cx["p"], cx["q"], cx["dl"], cx["nd"], cx["btile"], cx["bbt"]
            banks = []
            for j in range((nd + 1) // 2):
                ri = ring["i"] % NR
                ring["i"] += 1
                pi_ = pring["i"] % NRP
                pring["i"] += 1
                banks.append(pi_)
                ndj = min(2, nd - 2 * j)
                for dj in range(ndj):
                    di = 2 * j + dj
                    c = p + dl[di]
                    so = psS[ri][:, dj, :, :]
                    P.op("pe", lambda e, so=so, hp=hp, c=c, q=q: e.matmul(
                        so, lhsT=KT[:, hp, c * 128:(c + 1) * 128], rhs=QT[q][:, hp, :, :], start=True, stop=False),
                        r=[bKT[c // 8], bQT[q]], **({"w": [bpsS[ri]]} if dj == 0 else {"pw": [bpsS[ri]]}))
                    P.op("pe", lambda e, so=so, di=di, hp=hp, btile=btile: e.matmul(
                        so, lhsT=ident[:], rhs=btile[:, di, 2 * hp:2 * hp + 2, :], start=False, stop=True),
                        r=[b_ident, bbt], pw=[bpsS[ri]])
                act(Pt[pi_][:, 0:ndj], psS[ri][:, 0:ndj], ACTF.Exp, r=[bpsS[ri]], w=[bPt[pi_]], scale=0.125)
            return banks

        def partB(cx, hp, banks):
            p, dl, nd = cx["p"], cx["dl"], cx["nd"]
            for hh in range(2):
                h = 2 * hp + hh
                oq = h // 4
                for di in range(nd):
                    ri = banks[di // 2]
                    dj = di % 2
                    c = p + dl[di]
                    P.op("pe", lambda e, oq=oq, h=h, hh=hh, di=di, dj=dj, c=c, ri=ri, nd=nd: e.matmul(
                        psO[oq][:, h % 4, :], lhsT=Pt[ri][:, dj, hh, :], rhs=V[:, c, h * 65:(h + 1) * 65], start=(di == 0), stop=(di == nd - 1)),
                        r=[bPt[ri], bV[c // 8]], **({"w": [bpsO[oq]]} if (h % 4 == 0 and di == 0) else {"pw": [bpsO[oq]]}))

        def tail(cx):
            p, q = cx["p"], cx["q"]
            for oq in range(2):
                P.op("dve", lambda e, oq=oq: e.reciprocal(out=den[:, oq * 4:(oq + 1) * 4], in_=psO[oq][:, :, 64]), r=[bpsO[oq]],
                     **({"w": [bden]} if oq == 0 else {"pw": [bden]}))
            for oq in range(2):
                P.op("dve", lambda e, oq=oq: e.tensor_tensor(out=yat[:, oq * 256:(oq + 1) * 256].rearrange("p (h d) -> p h d", d=64),
                                                             in0=psO[oq][:, :, 0:64],
                                                             in1=den[:, oq * 4:(oq + 1) * 4].unsqueeze(2).to_broadcast([128, 4, 64]), op=ALU.mult),
                     r=[bpsO[oq], bden], **({"w": [byat]} if oq == 0 else {"pw": [byat]}))
            P.op("act", lambda e: e.activation(out=junk[:], in_=yat[:], func=ACTF.Square, accum_out=ssa[:, 0:1]), r=[byat], w=[bjunk, bssa])
            P.op("act", lambda e: e.activation(out=ssa[:, 1:2], in_=ssa[:, 0:1], func=ACTF.Sqrt, scale=1.0 / 512, bias=epsc[:]),
                 r=[bssa, b_eps], w=[bssa])
            P.op("dve", lambda e: e.reciprocal(out=ssa[:, 2:3], in_=ssa[:, 1:2]), r=[bssa], w=[bssa])
            P.op("pool", lambda e: e.tensor_tensor(out=ya2[:], in0=yat[:], in1=ona[:], op=ALU.mult), r=[byat, bona], w=[bya2])
            P.op("dve", lambda e: e.scalar_tensor_tensor(out=yab[:], in0=ya2[:], scalar=ssa[:, 2:3], in1=Gt[q][:], op0=ALU.mult, op1=ALU.mult),
                 r=[bya2, bssa, bGt[q]], w=[byab])

        def tail2(cx):
            p = cx["p"]
            for k in range(4):
                P.op("pe", lambda e, k=k: e.transpose(out=pta[:, k, :], in_=yab[:, k * 128:(k + 1) * 128], identity=ident[:]),
                     r=[byab, b_ident], **({"w": [bpta]} if k == 0 else {"pw": [bpta]}))
            ob = (p // 2) % 2
            oi = p % 2
            P.op("dve", lambda e, ob=ob, oi=oi: e.tensor_copy(out=ATo[ob][:, :, oi * 128:(oi + 1) * 128], in_=pta[:]), r=[bpta],
                 **({"w": [bATo[ob]]} if oi == 0 else {"pw": [bATo[ob]]}))
            if oi == 1:
                t0 = (p - 1) * 128
                P.dma(SQ, ATD[:, :, t0:t0 + 256].rearrange("a p t -> p a t"), ATo[ob][:], r=[bATo[ob]], pw=[bD["ATD"]])

        units = [(p, hp) for p in range(64) for hp in range(4)]
        cxs = {}
        prev = None
        for u in units + [None]:
            cur = None
            if u is not None:
                p, hp = u
                if hp == 0:
                    cxs[p] = setup(p)
                    ensure_kv((p + 3) // 8 + 1)
                cur = (p, hp, partA(cxs[p], hp))
            if prev is not None:
                pp, php, pbanks = prev
                partB(cxs[pp], php, pbanks)
                if php == 3:
                    tail(cxs[pp])
                if php == 1 and pp >= 1:
                    tail2(cxs[pp - 1])
            prev = cur
        tail2(cxs[63])
        P.barrier()
        P.release()

    def stage_out(l, xsrc, bxsrc, xdst, bxdst):
        P.mark()
        wo = P.sb("wo", [128, 8, 1024], BF16); bwo = Buf()
        wst = [P.sb("wost", [128, 8, 128], F32) for _ in range(2)]; bwst = [Buf(), Buf()]
        wv = D["w_out"][l].rearrange("(k p) c -> p k c", p=128)
        for j in range(8):
            q = j % 2
            P.dma("sp", wst[q][:], wv[:, :, j * 128:(j + 1) * 128], w=[bwst[q]])
            P.op("act", lambda e, q=q, j=j: e.activation(out=wo[:, :, j * 128:(j + 1) * 128], in_=wst[q][:], func=ACTF.Copy), r=[bwst[q]], pw=[bwo])
        sm = P.sb("smo", [128, 8], F32); bsm = Buf()
        P.dma("sp", sm[:, 0:4], D["hy_bias"][l], pw=[bsm])
        P.dma("sp", sm[:, 4:8], D["on_hy"][l], pw=[bsm])
        ones = P.sb("ones", [128, 128], BF16); bones = Buf()
        P.op("pool", lambda e: e.memset(ones[:], 1.0), w=[bones])
        NBK = 16
        x0t = [P.sb("x0t", [128, 4, 512], BF16) for _ in range(2)]; bx0 = [Buf(), Buf()]
        ut = [P.sb("ut", [128, 4, 512], BF16) for _ in range(2)]; but = [Buf(), Buf()]
        gtt = [P.sb("gtt", [128, 4, 512], BF16) for _ in range(2)]; bgt = [Buf(), Buf()]
        yt = [P.sb("yt", [128, 4, 512], F32) for _ in range(2)]; byt = [Buf(), Buf()]
        AaT = [P.sb("AaT", [128, 4, 512], BF16) for _ in range(3)]; bAa = [Buf(), Buf(), Buf()]
        tmp = P.sb("tmpo", [128, 4, 512], F32); btmp = Buf()
        yraw = P.sb("yraw", [128, 4, 512], F32); byraw = Buf()
        sqh = P.sb("sqh", [128, 4, 512], BF16); bsqh = Buf()
        grs = P.sb("grs", [128, 4, 512], F32); bgrs = Buf()
        AhT = [P.sb("AhT", [128, 4, 512], BF16) for _ in range(2)]; bAh = [Buf(), Buf()]
        pss = P.ps("pss", [128, 512]); bpss = Buf()
        rsb = P.sb("rsb", [128, 512], F32); brsb = Buf()
        rsb2 = P.sb("rsb2", [128, 512], F32); brsb2 = Buf()
        psP = [P.ps("psP", [128, 512]) for _ in range(4)]; bpsP = [Buf() for _ in range(4)]
        xt = [P.sb("xto", [128, 4, 1024], F32) for _ in range(3)]; bxt = [Buf(), Buf(), Buf()]
        xo = [P.sb("xo", [128, 4, 1024], F32) for _ in range(2)]; bxo = [Buf(), Buf()]
        cm = lambda ap, t0: ap[:, t0:t0 + 512].rearrange("(k p) t -> p k t", p=128)

        def prepA(b):
            q = b % 2
            q3 = b % 3
            t0 = b * 512
            P.dma("sp", x0t[q][:], cm(x0D, t0), r=[bD["x0D"]], w=[bx0[q]])
            P.dma("sp", ut[q][:], cm(uD, t0), r=[bD["uD"]], w=[but[q]])
            P.dma("sp", gtt[q][:], cm(gD, t0), r=[bD["gD"]], w=[bgt[q]])
            P.dma("sp", yt[q][:], cm(yD, t0), r=[bD["yD"]], w=[byt[q]])
            P.dma("sp", AaT[q3][:], ATD[:, :, t0:t0 + 512].rearrange("a p t -> p a t"), r=[bD["ATD"]], w=[bAa[q3]])
            P.dma("sp", xt[q3][:], xsrc[t0:t0 + 512, :].rearrange("(t p) d -> p t d", p=128), r=[bxsrc], w=[bxt[q3]])
            for k in range(4):
                P.op("dve", lambda e, k=k: e.scalar_tensor_tensor(out=tmp[:, k, :], in0=ut[q][:, k, :], scalar=sm[:, k:k + 1], in1=yt[q][:, k, :],
                                                                  op0=ALU.mult, op1=ALU.add), r=[but[q], byt[q], bsm],
                     **({"w": [btmp]} if k == 0 else {"pw": [btmp]}))
            P.op("pool", lambda e: e.tensor_tensor(out=yraw[:], in0=tmp[:], in1=x0t[q][:], op=ALU.mult), r=[btmp, bx0[q]], w=[byraw])
            act(sqh[:], yraw[:], ACTF.Square, r=[byraw], w=[bsqh])

        def prepB(b):
            q = b % 2
            for k in range(4):
                P.op("pe", lambda e, k=k: e.matmul(pss[:], lhsT=ones[:], rhs=sqh[:, k, :], start=(k == 0), stop=(k == 3)),
                     r=[bsqh, bones], **({"w": [bpss]} if k == 0 else {"pw": [bpss]}))
            P.op("act", lambda e: e.activation(out=rsb[:], in_=pss[:], func=ACTF.Sqrt, scale=1.0 / 512, bias=epsc[:]),
                 r=[bpss, b_eps], w=[brsb])
            P.op("dve", lambda e: e.reciprocal(out=rsb2[:], in_=rsb[:]), r=[brsb], w=[brsb2])
            P.op("pool", lambda e: e.tensor_tensor(out=grs[:], in0=gtt[q][:], in1=rsb2[:].unsqueeze(1).to_broadcast([128, 4, 512]), op=ALU.mult),
                 r=[bgt[q], brsb2], w=[bgrs])
            for k in range(4):
                P.op("dve", lambda e, k=k: e.scalar_tensor_tensor(out=AhT[q][:, k, :], in0=yraw[:, k, :], scalar=sm[:, 4 + k:5 + k], in1=grs[:, k, :],
                                                                  op0=ALU.mult, op1=ALU.mult), r=[byraw, bgrs, bsm],
                     **({"w": [bAh[q]]} if k == 0 else {"pw": [bAh[q]]}))

        def tiles(b):
            q = b % 2
            q3 = b % 3
            for tt_ in range(4):
                tg = b * 4 + tt_
                xq = tg % 2
                ts_ = slice(tt_ * 128, (tt_ + 1) * 128)
                for half in range(2):
                    pi = xq * 2 + half
                    for k in range(8):
                        src, bs = (AhT[q], bAh[q]) if k < 4 else (AaT[q3], bAa[q3])
                        P.op("pe", lambda e, src=src, k=k, pi=pi, half=half, ts_=ts_: e.matmul(
                            psP[pi][:], lhsT=src[:, k % 4, ts_], rhs=wo[:, k, half * 512:(half + 1) * 512], start=(k == 0), stop=(k == 7)),
                            r=[bs, bwo], **({"w": [bpsP[pi]]} if k == 0 else {"pw": [bpsP[pi]]}))
                for half in range(2):
                    pi = xq * 2 + half
                    hs = slice(half * 512, (half + 1) * 512)
                    P.op("dve", lambda e, pi=pi, hs=hs, tt_=tt_: e.tensor_tensor(out=xo[q][:, tt_, hs], in0=psP[pi][:], in1=xt[q3][:, tt_, hs], op=ALU.add),
                         r=[bpsP[pi], bxt[q3]], **({"w": [bxo[q]]} if (half == 0 and tt_ == 0) else {"pw": [bxo[q]]}))
            P.dma(SQ, xdst[b * 512:(b + 1) * 512, :].rearrange("(t p) d -> p t d", p=128), xo[q][:], r=[bxo[q]], pw=[bxdst])

        prepA(0)
        prepB(0)
        prepA(1)
        for b in range(NBK):
            if b + 1 < NBK:
                prepB(b + 1)
            if b + 2 < NBK:
                prepA(b + 2)
            tiles(b)
        P.barrier()
        P.release()

    stages = build.stages
    for l in range(DEPTH):
        xsrc, bxs = (D["x"], Buf()) if l == 0 else (x1, bD["x1"])
        xdst, bxd = (x1, bD["x1"]) if l == 0 else (y_out, bD["y"])
        if "filter" in stages:
            stage_filter(l)
        if "proj" in stages:
            stage_proj(l, xsrc, bxs)
        if "ffft" in stages or "conv" in stages:
            P.mark()
            C, bc = fft_consts()
            if "ffft" in stages:
                stage_filter_fft(l, C, bc)
            if "conv" in stages:
                stage_conv(l, C, bc)
            P.release()
        if "attn" in stages:
            stage_attn(l)
        if "out" in stages:
            stage_out(l, xsrc, bxs, xdst, bxd)
        if build.one_layer:
            break
    P.barrier()
    P.emit()
    return nc


build.stages = ("filter", "proj", "ffft", "conv", "attn", "out")
build.one_layer = False
build.parts = ('hy', 'att')


def _dup(a):
    return np.ascontiguousarray(np.concatenate([a, a], 1)[:, :, None])


def _pad_w1(w):
    z = np.zeros_like(w)
    return np.ascontiguousarray(np.stack([np.concatenate([w, z], 2), np.concatenate([z, w], 2)], 1))


def _blockdiag(w):
    out = np.zeros((w.shape[0], 128, 128), np.float32)
    out[:, :64, :64] = w
    out[:, 64:, 64:] = w
    return out


def _pad_w3(w):
    out = np.zeros((w.shape[0], 128, 1024), np.float32)
    out[:, :64, :512] = w[:, :, :512]
    out[:, 64:, 512:] = w[:, :, 512:]
    return out

def make_in_maps(inp):
    f = lambda a: np.ascontiguousarray(np.asarray(a, dtype=np.float32))
    xp, xs = f(inp["x_prompt"]), f(inp["x_sample"])
    slots = [xp[0], xp[1], xp[2], xp[3], xs[0:2].reshape(T, 1024), xs[2:4].reshape(T, 1024)]
    slots += [slots[4], slots[5]]
    isp = [True] * 4 + [False] * 4
    cP, cS = _consts(True), _consts(False)
    bc128 = lambda a: np.ascontiguousarray(np.broadcast_to(a[:, None, :], (a.shape[0], 128, a.shape[1])))
    chan = lambda a, k: np.ascontiguousarray(a.reshape(a.shape[0], k, 128).transpose(0, 2, 1))
    cw = f(inp["conv_w"])
    shared = {
        "norm_g": bc128(f(inp["norm_g"])),
        "w_in": f(inp["w_in"]),
        "conv_w": np.ascontiguousarray(cw.reshape(DEPTH, 3, 12, 128).transpose(0, 3, 2, 1)),
        "conv_b": chan(f(inp["conv_b"]), 12),
        "f_w1": _pad_w1(f(inp["f_w1"])),
        "f_b1": _dup(f(inp["f_b1"])),
        "f_fr1": _dup(f(inp["f_fr1"])),
        "f_w2": _blockdiag(f(inp["f_w2"])),
        "f_b2": _dup(f(inp["f_b2"])),
        "f_fr2": _dup(f(inp["f_fr2"])),
        "f_w3": _pad_w3(f(inp["f_w3"])),
        "hy_bias": chan(f(inp["hy_bias"]), 4),
        "on_hy": chan(f(inp["on_hy"]), 4),
        "qn_g": bc128(f(inp["qn_g"])),
        "kn_g": bc128(f(inp["kn_g"])),
        "on_att": bc128(f(inp["on_att"])),
        "rpbT": np.stack([_rpb_gather(f(inp["rpb"])[l]) for l in range(DEPTH)]),
        "w_out": f(inp["w_out"]),
    }
    maps = []
    for c in range(8):
        m = dict(shared)
        m.update(cP if isp[c] else cS)
        m["x"] = np.ascontiguousarray(slots[c])
        maps.append(m)
    return maps


_NC = {}


def kernel(**inputs):
    if "nc" not in _NC:
        _NC["nc"] = build()
    maps = make_in_maps(inputs)
    res = run_bass_kernel_spmd(_NC["nc"], maps, core_ids=list(range(8)))
    ys = [np.asarray(res.results[c]["y"], dtype=np.float32) for c in range(6)]
    y_prompt = np.stack(ys[0:4])
    y_sample = np.concatenate([ys[4].reshape(2, SEG, 1024), ys[5].reshape(2, SEG, 1024)], 0)
    return (y_prompt, y_sample)
```
